# Optimizing a Trainium2 kernel written in Bass

```python
import math
import jax, jax.numpy as jnp
from jax import lax
import numpy as np

D_MODEL = 1024
BATCH = 8
SEQ = 8192
DEPTH = 1
DEC_BATCH = 32
DEC_SEQ = 32
PAST_LEN = 1024

CHUNK = 64
Q_BLOCK = 128
MIX_WIDTH = D_MODEL
CONV_CH = MIX_WIDTH // 2
CONV_GROUPS = 8
CONV_K = 3
N_ATT_HEADS = 4
DV = (MIX_WIDTH - CONV_CH) // N_ATT_HEADS
DQK = DV // 2
D_FF = 4 * D_MODEL
EPS = 1e-5
NEG_INF = -1e30
ATT_QK = N_ATT_HEADS * 2 * DQK
ATT_V = N_ATT_HEADS * DV
IN_WIDTH = 3 * CONV_CH + 2 * ATT_QK + ATT_V
SPLITS = [CONV_CH, 2 * CONV_CH, 3 * CONV_CH, 3 * CONV_CH + ATT_QK, 3 * CONV_CH + 2 * ATT_QK]

kernel_name = "hybrid_shortconv_diffattn_stream_step"


def rmsnorm(x, g):
    xf = x.astype(jnp.float32)
    y = xf * lax.rsqrt(jnp.mean(xf * xf, axis=-1, keepdims=True) + EPS) * g.astype(jnp.float32)
    return y.astype(x.dtype)


def alibi_slopes(n_heads):
    return jnp.asarray([2.0 ** (-8.0 * (i + 1) / n_heads) for i in range(n_heads)], dtype=jnp.float32)


def lambda_init_fn(layer_idx):
    return 0.8 - 0.6 * math.exp(-0.3 * layer_idx)


def diff_attention(q, k, v, q_pos, k_pos, lam):
    b, tq = q.shape[0], q.shape[1]
    slopes = alibi_slopes(N_ATT_HEADS)
    scale = DQK ** -0.5
    vf = v.astype(jnp.float32)

    def attend(args):
        qb, pb = args
        dist = jnp.abs(pb[:, None] - k_pos[None, :]).astype(jnp.float32)
        allowed = (k_pos[None, :] // CHUNK) <= (pb[:, None] // CHUNK)
        bias = -slopes[:, None, None] * dist[None]
        s = jnp.einsum('bqhcd,bkhcd->bhcqk', qb, k).astype(jnp.float32) * scale + bias[None, :, None]
        s = jnp.where(allowed[None, None, None], s, NEG_INF)
        p = jax.nn.softmax(s, axis=-1)
        w = p[:, :, 0] - lam * p[:, :, 1]
        return jnp.einsum('bhqk,bkhe->bqhe', w, vf).astype(q.dtype)

    if tq > Q_BLOCK and tq % Q_BLOCK == 0:
        nb = tq // Q_BLOCK
        qb = jnp.swapaxes(q.reshape(b, nb, Q_BLOCK, N_ATT_HEADS, 2, DQK), 0, 1)
        pb = q_pos.reshape(nb, Q_BLOCK)
        o = lax.map(attend, (qb, pb))
        return jnp.swapaxes(o, 0, 1).reshape(b, tq, N_ATT_HEADS, DV)
    return attend((q, q_pos))


def token_mix(h, w_in, conv_w, lq1, lk1, lq2, lk2, subln_g, w_o, conv_past, k_past, v_past,
              q_pos, k_pos, lam_init):
    b, t, _ = h.shape
    z = h @ w_in
    bg, cg, u, q, k, v = jnp.split(z, SPLITS, axis=-1)
    uc = cg * u
    padded = jnp.concatenate([conv_past.astype(uc.dtype), uc], axis=1)
    conv = conv_w[0] * padded[:, 0:t] + conv_w[1] * padded[:, 1:t + 1] + conv_w[2] * padded[:, 2:t + 2]
    y_conv = bg * conv
    new_conv = padded[:, -(CONV_K - 1):]
    k_new = k.reshape(b, t, N_ATT_HEADS, 2 * DQK)
    v_new = v.reshape(b, t, N_ATT_HEADS, DV)
    if k_past is None:
        k_all, v_all = k_new, v_new
    else:
        k_all = jnp.concatenate([k_past.astype(k_new.dtype), k_new], axis=1)
        v_all = jnp.concatenate([v_past.astype(v_new.dtype), v_new], axis=1)
    lam = (jnp.exp(jnp.sum(lq1.astype(jnp.float32) * lk1.astype(jnp.float32)))
           - jnp.exp(jnp.sum(lq2.astype(jnp.float32) * lk2.astype(jnp.float32))) + lam_init)
    o = diff_attention(q.reshape(b, t, N_ATT_HEADS, 2, DQK),
                       k_all.reshape(b, -1, N_ATT_HEADS, 2, DQK), v_all, q_pos, k_pos, lam)
    o = rmsnorm(o, subln_g) * (1.0 - lam_init)
    y = jnp.concatenate([y_conv, o.reshape(b, t, ATT_V)], axis=-1) @ w_o
    return y, k_new, v_new, new_conv


def run_layer(x, c, l, lam_init, norm1_g, norm2_g, w_ada, b_ada, w_in, conv_w, lambda_q1, lambda_k1,
              lambda_q2, lambda_k2, subln_g, w_o, w_mlp1, w_mlp2, conv_past, k_past, v_past, q_pos, k_pos):
    mod = jax.nn.silu(c) @ w_ada[l] + b_ada[l]
    sh1, sc1, g1, sh2, sc2, g2 = [m[:, None, :] for m in jnp.split(mod, 6, axis=-1)]
    h = rmsnorm(x, norm1_g[l]) * (1.0 + sc1) + sh1
    mix, k_new, v_new, conv_new = token_mix(h, w_in[l], conv_w[l], lambda_q1[l], lambda_k1[l],
                                            lambda_q2[l], lambda_k2[l], subln_g[l], w_o[l],
                                            conv_past, k_past, v_past, q_pos, k_pos, lam_init)
    x = x + g1 * mix
    h = rmsnorm(x, norm2_g[l]) * (1.0 + sc2) + sh2
    x = x + g2 * (jnp.square(jax.nn.relu(h @ w_mlp1[l])) @ w_mlp2[l])
    return x, k_new, v_new, conv_new


def setup_inputs(seed: int = 0) -> dict:
    key = jax.random.key(seed)
    ks = jax.random.split(key, 24)
    f32 = jnp.float32
    nrm = lambda k, shape, s: jax.random.normal(k, shape, f32) * s
    return {
        "x_prompt": nrm(ks[0], (BATCH, SEQ, D_MODEL), 1.0),
        "x_sample": nrm(ks[1], (DEC_BATCH, DEC_SEQ, D_MODEL), 1.0),
        "cache_k": nrm(ks[2], (DEPTH, DEC_BATCH, PAST_LEN, N_ATT_HEADS, 2 * DQK), 1.0),
        "cache_v": nrm(ks[3], (DEPTH, DEC_BATCH, PAST_LEN, N_ATT_HEADS, DV), 1.0),
        "state_conv": nrm(ks[4], (DEPTH, DEC_BATCH, CONV_K - 1, CONV_CH), 1.0),
        "c_prompt": nrm(ks[5], (BATCH, D_MODEL), 1.0),
        "c_sample": nrm(ks[6], (DEC_BATCH, D_MODEL), 1.0),
        "norm1_g": 1.0 + nrm(ks[7], (DEPTH, D_MODEL), 0.02),
        "norm2_g": 1.0 + nrm(ks[8], (DEPTH, D_MODEL), 0.02),
        "w_ada": nrm(ks[9], (DEPTH, D_MODEL, 6 * D_MODEL), 0.5 * D_MODEL ** -0.5),
        "b_ada": nrm(ks[10], (DEPTH, 6 * D_MODEL), 0.01),
        "w_in": nrm(ks[11], (DEPTH, D_MODEL, IN_WIDTH), D_MODEL ** -0.5),
        "conv_w": nrm(ks[12], (DEPTH, CONV_K, CONV_CH), CONV_K ** -0.5),
        "lambda_q1": nrm(ks[13], (DEPTH, DQK), 0.1),
        "lambda_k1": nrm(ks[14], (DEPTH, DQK), 0.1),
        "lambda_q2": nrm(ks[15], (DEPTH, DQK), 0.1),
        "lambda_k2": nrm(ks[16], (DEPTH, DQK), 0.1),
        "subln_g": 1.0 + nrm(ks[17], (DEPTH, DV), 0.02),
        "w_o": nrm(ks[18], (DEPTH, MIX_WIDTH, D_MODEL), MIX_WIDTH ** -0.5),
        "w_mlp1": nrm(ks[19], (DEPTH, D_MODEL, D_FF), D_MODEL ** -0.5),
        "w_mlp2": nrm(ks[20], (DEPTH, D_FF, D_MODEL), D_FF ** -0.5),
        "final_g": 1.0 + nrm(ks[21], (D_MODEL,), 0.02),
    }


def reference(x_prompt, x_sample, cache_k, cache_v, state_conv, c_prompt, c_sample,
              norm1_g, norm2_g, w_ada, b_ada, w_in, conv_w, lambda_q1, lambda_k1, lambda_q2, lambda_k2,
              subln_g, w_o, w_mlp1, w_mlp2, final_g):
    b_p, t_p = x_prompt.shape[0], x_prompt.shape[1]
    t_s = x_sample.shape[1]
    past = cache_k.shape[2]
    pos_p = jnp.arange(t_p, dtype=jnp.int32)
    q_pos_s = past + jnp.arange(t_s, dtype=jnp.int32)
    k_pos_s = jnp.arange(past + t_s, dtype=jnp.int32)
    weights = (norm1_g, norm2_g, w_ada, b_ada, w_in, conv_w, lambda_q1, lambda_k1,
               lambda_q2, lambda_k2, subln_g, w_o, w_mlp1, w_mlp2)
    xp, xs = x_prompt, x_sample
    kp_l, vp_l, cp_l, ksl, vsl, csl = [], [], [], [], [], []
    for l in range(DEPTH):
        lam_init = lambda_init_fn(l)
        zero_pad = jnp.zeros((b_p, CONV_K - 1, CONV_CH), xp.dtype)
        xp, kp, vp, cp = run_layer(xp, c_prompt, l, lam_init, *weights, zero_pad, None, None, pos_p, pos_p)
        xs, kn, vn, cn = run_layer(xs, c_sample, l, lam_init, *weights, state_conv[l], cache_k[l], cache_v[l],
                                   q_pos_s, k_pos_s)
        kp_l.append(kp); vp_l.append(vp); cp_l.append(cp)
        ksl.append(kn); vsl.append(vn); csl.append(cn)
    y_prompt = rmsnorm(xp, final_g)
    y_sample = rmsnorm(xs, final_g)
    return (y_prompt, y_sample, jnp.stack(kp_l), jnp.stack(vp_l), jnp.stack(cp_l),
            jnp.stack(ksl), jnp.stack(vsl), jnp.stack(csl))
```

```python
import math
from contextlib import ExitStack

import numpy as np
import concourse.bass as bass
import concourse.mybir as mybir
from concourse.bass_utils import run_bass_kernel_spmd

F32 = mybir.dt.float32
BF16 = mybir.dt.bfloat16
ALU = mybir.AluOpType
AF = mybir.ActivationFunctionType
AX = mybir.AxisListType

EPS = 1e-5
D = 1024
NH = 4
CONV_CH = 512
DFF = 4096
SLOPES = [2.0 ** (-8.0 * (i + 1) / NH) for i in range(NH)]
LAM_INIT = 0.8 - 0.6 * math.exp(-0.3 * 0)
BIG = 30000.0
DEC_SEQ = 32
NSS = 4
TABOFF = 3
DBG = {"stop": None}


class _Stop(Exception):
    pass


def _chk(n):
    if DBG["stop"] is not None and DBG["stop"] == n:
        raise _Stop()


class Buf:
    __slots__ = ("name", "writers", "readers", "prev_readers", "dsem", "dcount", "psum")

    def __init__(self, name):
        self.name = name
        self.psum = False
        self.writers = []
        self.readers = []
        self.prev_readers = []
        self.dsem = None
        self.dcount = 0


class Op:
    __slots__ = ("eng", "fn", "deps", "is_dma", "sem", "val", "signal", "idx")


class Prog:
    ENGS = ("sp", "pool", "act", "dve", "pe")

    def __init__(self, nc, stack):
        self.nc = nc
        self.stack = stack
        self.ops = {e: [] for e in self.ENGS}
        self.all = []
        self.dma_bufs = []
        self.eng_sem = {e: stack.enter_context(nc.semaphore("es_" + e)) for e in self.ENGS}

    def buf(self, name):
        return Buf(name)

    def op(self, eng, fn, reads=(), writes=(), pwrites=(), dma_buf=None):
        o = Op()
        o.eng = eng
        o.fn = fn
        o.is_dma = dma_buf is not None
        o.deps = {}
        o.signal = False
        o.sem = None
        o.val = 0

        def add(d, raw):
            if d is o:
                return
            need = d.is_dma or o.is_dma or (d.eng != eng) or raw or eng != "pe"
            o.deps[d] = o.deps.get(d, False) or need

        for b in reads:
            for w in b.writers:
                add(w, True)
            if b.psum:
                for r in b.readers:
                    if r.eng != eng:
                        add(r, True)
        for b in writes:
            if b.readers:
                for r in b.readers:
                    add(r, False)
            else:
                for w in b.writers:
                    add(w, False)
                for r in b.prev_readers:
                    add(r, False)
        for b in pwrites:
            if b.readers:
                for r in b.readers:
                    add(r, False)
            else:
                for r in b.prev_readers:
                    add(r, False)
        for b in reads:
            b.readers.append(o)
        for b in writes:
            if b.readers:
                b.prev_readers = b.readers
                b.readers = []
            b.writers = [o]
        for b in pwrites:
            if b.readers:
                b.prev_readers = b.readers
                b.readers = []
                b.writers = [o]
            else:
                b.writers.append(o)
        if o.is_dma:
            if dma_buf.dsem is None:
                dma_buf.dsem = {}
            if eng not in dma_buf.dsem:
                sem = self.stack.enter_context(self.nc.semaphore("ds_%s_%s" % (dma_buf.name, eng)))
                dma_buf.dsem[eng] = [sem, 0]
                self.dma_bufs.append(dma_buf.dsem[eng])
            ent = dma_buf.dsem[eng]
            ent[1] += 1
            o.sem = ent[0]
            o.val = 16 * ent[1]
        self.ops[eng].append(o)
        self.all.append(o)
        return o

    def emit(self):
        nc = self.nc
        for e in self.ENGS:
            for i, o in enumerate(self.ops[e]):
                o.idx = i
        for o in self.all:
            last = {}
            nd = {}
            for d, need in o.deps.items():
                if not need:
                    continue
                if d.is_dma:
                    nd[d] = True
                else:
                    if d.eng not in last or last[d.eng].idx < d.idx:
                        last[d.eng] = d
            for d in last.values():
                nd[d] = True
            o.deps = nd
            for d in nd:
                d.signal = True
        for e in self.ENGS:
            cnt = 0
            for o in self.ops[e]:
                if (not o.is_dma) and o.signal:
                    cnt += 1
                    o.val = cnt
                    o.sem = self.eng_sem[e]

        def run(e, eng):
            waited = {}
            for o in self.ops[e]:
                need_w = {}
                for d, need in o.deps.items():
                    if not need:
                        continue
                    k = id(d.sem)
                    if k not in need_w or need_w[k][1] < d.val:
                        need_w[k] = (d.sem, d.val)
                for k, (sem, val) in need_w.items():
                    if waited.get(k, 0) < val:
                        eng.wait_ge(sem, val)
                        waited[k] = val
                ins = o.fn(eng)
                if o.is_dma:
                    ins.then_inc(o.sem, 16)
                elif o.signal:
                    ins.then_inc(o.sem, 1)
            if e == "pool":
                for (sem, cnt) in self.dma_bufs:
                    eng.wait_ge(sem, 16 * cnt)

        with nc.Block() as block:
            @block.sync
            def _(eng):
                run("sp", eng)

            @block.gpsimd
            def _(eng):
                run("pool", eng)

            @block.scalar
            def _(eng):
                run("act", eng)

            @block.vector
            def _(eng):
                run("dve", eng)

            @block.tensor
            def _(eng):
                run("pe", eng)


def build(SEQ, PAST):
    NB = SEQ // 512
    PT = PAST // 128
    assert SEQ % 512 == 0 and PAST % 128 == 0
    NTAB = max(4 * NB, PT) + TABOFF + 1

    nc = bass.Bass("TRN2", target_bir_lowering=False)
    stack = ExitStack()
    P = Prog(nc, stack)

    def din(name, shape, dt=F32):
        return nc.dram_tensor(name, shape, dt, kind="ExternalInput").ap()

    def dout(name, shape, dt=F32):
        return nc.dram_tensor(name, shape, dt, kind="ExternalOutput").ap()

    def dscr(name, shape, dt):
        return nc.dram_tensor(name, shape, dt, kind="Internal").ap()

    xp = din("xp", [SEQ, D])
    xs = din("xs", [NSS * DEC_SEQ, D])
    ck = din("ck", [NSS * PAST, 512])
    cv = din("cv", [NSS * PAST, 512])
    sconv = din("sconv", [NSS * 2, 512])
    cc = din("cc", [1 + NSS, D])
    norm1_g = din("norm1_g", [8, 128])
    norm2_g = din("norm2_g", [8, 128])
    w_ada = din("w_ada", [D, 6 * D])
    b_ada = din("b_ada", [6 * D])
    w_in = din("w_in", [768, 4096])
    conv_w = din("conv_w", [12, 128])
    lam_in = [din("lq1", [64]), din("lk1", [64]), din("lq2", [64]), din("lk2", [64])]
    subln_g = din("subln_g", [1, 128])
    w_o = din("w_o", [256, 4096])
    w_mlp1 = din("w_mlp1", [1024, 4096])
    w_mlp2 = din("w_mlp2", [1024, 4096])
    final_g = din("final_g", [D])

    y_p = dout("y_p", [SEQ, D])
    y_s = dout("y_s", [NSS * DEC_SEQ, D])
    k_p = dout("k_p", [SEQ, 512])
    v_p = dout("v_p", [SEQ, 512])
    conv_p = dout("conv_p", [2, 512])
    k_s = dout("k_s", [NSS * DEC_SEQ, 512])
    v_s = dout("v_s", [NSS * DEC_SEQ, 512])
    conv_s = dout("conv_s", [NSS * 2, 512])

    wb_in = dscr("wb_in", [768, 4096], BF16)
    wb_o = dscr("wb_o", [256, 4096], BF16)
    wb_1 = dscr("wb_1", [1024, 4096], BF16)
    wb_2 = dscr("wb_2", [1024, 4096], BF16)
    kt_scr = dscr("kt_scr", [NB * 128, 2048], BF16)
    v_scr = dscr("v_scr", [NB * 128, 2048], BF16)
    wb_in_m = wb_in.rearrange("a b -> (a b)").rearrange("(r n) -> r n", n=3072)
    wb_o_m = wb_o.rearrange("a b -> (a b)").rearrange("(r n) -> r n", n=1024)
    wb_1_m = wb_1
    wb_2_m = wb_2.rearrange("a b -> (a b)").rearrange("(r n) -> r n", n=1024)

    def T(name, shape, dt):
        return stack.enter_context(nc.sbuf_tensor(name, shape, dt))

    PS = [stack.enter_context(nc.psum_tensor("ps%d" % i, [128, 512], F32)) for i in range(8)]
    PSB = [P.buf("ps%d" % i) for i in range(8)]
    for _b in PSB:
        _b.psum = True
    ps_rr = [0]

    def next_bank():
        i = ps_rr[0]
        ps_rr[0] = (i + 1) % 8
        return i

    xb = T("xb", [128, 4 * D], F32)
    xb2d = T("xb2d", [128, D], F32)
    XBs = [[[P.buf("xb%d_%d_%d" % (st_, t, hf)) for hf in range(2)] for t in range(4)] for st_ in range(2)]
    stats = T("stats", [128, 64], F32)
    STB = [P.buf("stat%d" % i) for i in range(64)]
    xn = [T("xn%d" % i, [128, D], BF16) for i in range(4)]
    XN = [P.buf("xn%d" % i) for i in range(4)]
    hT = [T("hT%d" % i, [128, 512], BF16) for i in range(8)]
    HT = [P.buf("hT%d" % i) for i in range(8)]
    uS = [T("uS%d" % i, [128, 512], F32) for i in range(2)]
    US = [P.buf("uS%d" % i) for i in range(2)]
    uc = [T("uc%d" % i, [128, 520], F32) for i in range(2)]
    UC = [P.buf("uc%d" % i) for i in range(2)]
    cvt = [T("cvt%d" % i, [128, 512], F32) for i in range(2)]
    CVT = [P.buf("cvt%d" % i) for i in range(2)]
    cst = T("cst", [128, 4 * 8], F32)
    CST = [P.buf("cst%d" % j) for j in range(4)]
    QTc = [T("QT%d" % c, [66, 4 * 512], BF16) for c in range(2)]
    QTB = [P.buf("QT%d" % h) for h in range(4)]
    KTblk = [T("KTblk%d" % c, [66, 4 * 512], BF16) for c in range(2)]
    KTB = P.buf("KTblk")
    Vblk = T("Vblk", [128, 4 * 512], BF16)
    VB = P.buf("Vblk")
    kf = [T("kf%d" % i, [128, 512], F32) for i in range(2)]
    KF = [P.buf("kf%d" % i) for i in range(2)]
    vf = [T("vf%d" % i, [128, 512], F32) for i in range(2)]
    VF = [P.buf("vf%d" % i) for i in range(2)]
    qrow = kf[0]
    qrowb = Vblk
    kb = [T("kb%d" % i, [128, 512], BF16) for i in range(2)]
    KB = [P.buf("kb%d" % i) for i in range(2)]
    mixT = [T("mixT%d" % i, [128, 512], BF16) for i in range(8)]
    MIX = [P.buf("mixT%d" % i) for i in range(8)]
    pT = [T("pT%d" % i, [128, 512], BF16) for i in range(4)]
    PTB = [P.buf("pT%d" % i) for i in range(4)]
    NSL = 4
    sK = [[T("sK%d_%d" % (i, c), [66, 512], BF16) for c in range(2)] for i in range(NSL)]
    sV = [T("sV%d" % i, [128, 512], BF16) for i in range(NSL)]
    SL = [P.buf("sl%d" % i) for i in range(NSL)]
    at = [uS[0], uS[1], cvt[0], cvt[1], uc[0]]
    AT = [US[0], US[1], CVT[0], CVT[1], UC[0]]
    sqb = T("sqb", [128, 512], BF16)
    SQB = P.buf("sqb")
    HID = T("HID", [128, 32 * 512], BF16)
    HB = [P.buf("hid%d" % i) for i in range(32)]
    G1 = [T("G1_%d" % i, [128, D], F32) for i in range(2)]
    G2 = [T("G2_%d" % i, [128, D], F32) for i in range(2)]
    GB = P.buf("gates")
    GBs = P.buf("gates_s")
    FG = T("FG", [128, D], F32)
    tmpr = T("tmpr", [128, 512], F32)
    TMPR = P.buf("tmpr")
    cfo = tmpr
    stS = tmpr[0:8, :]
    rl = [T("rl%d" % i, [128, 512], F32) for i in range(2)]
    RL = [P.buf("rl%d" % i) for i in range(2)]
    yout = T("yout", [128, D], F32)
    YOUT = P.buf("yout")
    NW = 3
    wsl = [T("wsl%d" % i, [128, 4096], BF16) for i in range(NW)]
    WS = [P.buf("wsl%d" % i) for i in range(NW)]
    w_rr = [0]

    ident_f = T("ident_f", [128, 128], F32)
    identb = T("identb", [128, 128], BF16)
    ones_f = T("ones_f", [128, 128], F32)
    onesm_b = T("onesm_b", [128, 128], BF16)
    mhalf = T("mhalf", [128, 1], F32)
    epsT = T("epsT", [128, 1], F32)
    iot = T("iot", [128, 128], F32)
    tab = T("tab", [128, 4 * NTAB], F32)
    tabS = T("tabS", [128, 16], F32)
    E0b = T("E0b", [128, 4 * 128], BF16)
    Esb = T("Esb", [128, 16 * 32], BF16)
    e0f = T("e0f", [128, 128], F32)
    msk = T("msk", [128, 32], F32)
    msk2 = T("msk2", [128, 32], F32)
    stage = T("stage", [128, 128], F32)
    prmT = T("prmT", [128, 128], F32)
    siluT = T("siluT", [128, 40], BF16)
    modT = T("modT", [128, 160], F32)
    GS = T("GS", [128, 80], F32)
    selp = T("selp", [8, 128], F32)
    sels = T("sels", [8, 128], F32)
    selt = T("selt", [8, 128], F32)
    selt2 = T("selt2", [8, 128], F32)
    lamt = T("lamt", [128, 4 * 64], F32)
    lamp = T("lamp", [128, 64], F32)
    lams = T("lams", [128, 4], F32)
    neg_lam = T("neg_lam", [128, 1], F32)
    gsub = T("gsub", [128, 1], F32)
    badc = [T("badc%d" % i, [8, 512], F32) for i in range(2)]
    BADC = [P.buf("badc%d" % i) for i in range(2)]
    cfin = T("cfin", [128, 32], F32)
    CONSTB = P.buf("const")
    PRM = P.buf("prm")
    LAMB = P.buf("lam")
    MISC = P.buf("misc")
    CFIN = P.buf("cfin")
    CFO = TMPR

    modS = HID[0:8, 0:12288].bitcast(F32)
    MODB = HB[0:24]
    Kraw = HID[:, 0:PT * 512]
    KRB = HB[0:PT]
    Vsm = HID[:, 8 * 512:8 * 512 + PT * 512]
    VSB = HB[8:8 + PT]
    KTs = [HID[0:66, 16 * 512 + c * 4 * PT * 128:16 * 512 + (c + 1) * 4 * PT * 128] for c in range(2)]
    KSB = HB[16:16 + 2 * PT]

    def ACT(out, in_, func, reads, writes=(), pwrites=(), bias=None, scale=None, accum=None):
        def fn(e):
            kw = {}
            if bias is not None:
                kw["bias"] = bias
            if scale is not None:
                kw["scale"] = scale
            if accum is not None:
                kw["accum_out"] = accum
            return e.activation(out=out, in_=in_, func=func, **kw)
        return P.op("act", fn, reads, writes, pwrites)

    def TS(eng, out, in0, s1, s2, op0, op1, reads, writes=(), pwrites=()):
        def fn(e):
            if s2 is None:
                return e.tensor_scalar(out=out, in0=in0, scalar1=s1, scalar2=None, op0=op0)
            return e.tensor_scalar(out=out, in0=in0, scalar1=s1, scalar2=s2, op0=op0, op1=op1)
        return P.op(eng, fn, reads, writes, pwrites)

    def TT(eng, out, in0, in1, op, reads, writes=(), pwrites=()):
        def fn(e):
            return e.tensor_tensor(out=out, in0=in0, in1=in1, op=op)
        return P.op(eng, fn, reads, writes, pwrites)

    def STT(out, in0, scalar, in1, op0, op1, reads, writes=(), pwrites=()):
        def fn(e):
            return e.scalar_tensor_tensor(out=out, in0=in0, scalar=scalar, in1=in1, op0=op0, op1=op1)
        return P.op("dve", fn, reads, writes, pwrites)

    def CP(eng, out, in_, reads, writes=(), pwrites=()):
        def fn(e):
            return e.tensor_copy(out=out, in_=in_)
        return P.op(eng, fn, reads, writes, pwrites)

    def MSET(eng, ap, val, writes=(), pwrites=()):
        def fn(e):
            return e.memset(ap, val)
        return P.op(eng, fn, (), writes, pwrites)

    def MM(out, lhsT, rhs, start, stop, reads, writes=(), pwrites=()):
        def fn(e):
            return e.matmul(out, lhsT, rhs, start=start, stop=stop, skip_group_check=True)
        return P.op("pe", fn, reads, writes, pwrites)

    def TR(out, in_, ident, reads, writes=(), pwrites=()):
        def fn(e):
            return e.transpose(out, in_, ident)
        return P.op("pe", fn, reads, writes, pwrites)

    def DMA(q, out, in_, reads, writes, dbuf, pwrites=()):
        def fn(e):
            return e.dma_start(out=out, in_=in_)
        return P.op(q, fn, reads, writes, pwrites, dma_buf=dbuf)

    def IOTA(out, pattern, base, cm, writes):
        def fn(e):
            return e.iota(out, pattern, base=base, channel_multiplier=cm,
                          allow_small_or_imprecise_dtypes=True)
        return P.op("pool", fn, (), writes)

    def wload(src3, k, n, wc):
        i = w_rr[0]
        w_rr[0] = (i + 1) % NW
        dst = wsl[i][:, :].rearrange("p (k n) -> p k n", k=k)
        DMA("sp", dst, src3, [wc], [WS[i]], WS[i])
        return i

    WC = [P.buf("wcast%d" % i) for i in range(4)]
    wada3 = w_ada.rearrange("(k p) n -> p k n", p=128)
    ada_slots = []

    def ada_load(n):
        i = w_rr[0]
        w_rr[0] = (i + 1) % NW
        dst = wsl[i][:, :].rearrange("p (k n) -> p k n", k=8)
        DMA("pool", dst, wada3[:, :, n * 512:(n + 1) * 512], [], [WS[i]], WS[i])
        return i

    _tb = {}

    def B(t):
        k = id(t)
        if k not in _tb:
            _tb[k] = P.buf("t%d" % len(_tb))
        return _tb[k]

    _tb[id(kf[0])] = KF[0]
    _tb[id(stS)] = TMPR
    _tb[id(Vblk)] = VB

    def Bs(*ts):
        return [B(t) for t in ts]

    for n in range(NW):
        ada_slots.append(ada_load(n))
    CASTS = ((w_in, wb_in, 768), (w_o, wb_o, 256), (w_mlp1, wb_1, 1024), (w_mlp2, wb_2, 1024))
    for wi in (0, 1, 2, 3):
        src, dst, rows = CASTS[wi]
        for r in range(0, rows, 128):
            DMA("pool", dst[r:r + 128, :], src[r:r + 128, :], [], [], WC[wi], [WC[wi]])
    MSET("dve", stage[:, :], 0.0, Bs(stage))
    DMA("sp", stage[0:8, :], norm1_g, [], Bs(stage), B(stage))
    DMA("sp", stage[8:16, :], norm2_g, [], [], B(stage), Bs(stage))
    DMA("sp", stage[16:28, :], conv_w, [], [], B(stage), Bs(stage))
    DMA("sp", stage[28:29, :], subln_g, [], [], B(stage), Bs(stage))
    for kc in range(8):
        DMA("sp", stage[32 + kc * 5:32 + kc * 5 + 5, :], cc[:, kc * 128:(kc + 1) * 128], [], [], B(stage),
            Bs(stage))
    for i in range(4):
        DMA("sp", lamt[:, i * 64:(i + 1) * 64], lam_in[i].partition_broadcast(128), [], [], B(lamt), Bs(lamt))
    DMA("sp", FG[:, :], final_g.partition_broadcast(128), [], Bs(FG), B(FG))
    DMA("sp", stS[:, :], sconv, [], Bs(stS), B(stS))


    IOTA(iot[:, :], [[1, 128]], 0, -1, Bs(iot))
    TS("dve", ident_f[:, :], iot[:, :], 0.0, None, ALU.is_equal, None, Bs(iot), Bs(ident_f))
    CP("dve", identb[:, :], ident_f[:, :], Bs(ident_f), Bs(identb))
    MSET("dve", ones_f[:, :], 1.0, Bs(ones_f))
    MSET("dve", onesm_b[:, :], 1.0 / 128.0, Bs(onesm_b))
    MSET("dve", mhalf[:, :], -0.5, Bs(mhalf))
    MSET("dve", epsT[:, :], EPS, Bs(epsT))
    MSET("dve", cst[:, :], 0.0, CST)
    tabi = T("tabi", [128, NTAB], F32)
    IOTA(tabi[:, :], [[-128, NTAB]], 128 * TABOFF, 1, Bs(tabi))
    for h in range(4):
        TS("dve", tab[:, h * NTAB:(h + 1) * NTAB], tabi[:, :], SLOPES[h], None, ALU.mult, None,
           Bs(tabi), [], Bs(tab))
    tabsi = T("tabsi", [128, 4], F32)
    IOTA(tabsi[:, :], [[-32, 4]], 0, 1, Bs(tabsi))
    for h in range(4):
        TS("dve", tabS[:, h * 4:(h + 1) * 4], tabsi[:, :], SLOPES[h], None, ALU.mult, None,
           Bs(tabsi), [], Bs(tabS))
    for h in range(4):
        TS("dve", e0f[:, :], iot[:, :], 0.0, 2.0 * SLOPES[h], ALU.min, ALU.mult, Bs(iot), Bs(e0f))
        MSET("dve", e0f[64:128, 0:64], -BIG, Bs(e0f))
        CP("dve", E0b[:, h * 128:(h + 1) * 128], e0f[:, :], Bs(e0f), [], Bs(E0b))
    esi = T("esi", [128, 32], F32)
    mska = T("mska", [128, 32], F32)
    for s in range(4):
        IOTA(esi[:, :], [[1, 32]], 32 * s, -1, Bs(esi))
        IOTA(mska[:, :], [[0, 32]], -32 * s, 1, Bs(mska))
        TS("dve", msk2[:, :], mska[:, :], 0.0, None, ALU.is_ge, None, Bs(mska), Bs(msk2))
        TS("dve", msk[:, :], mska[:, :], 32.0, None, ALU.is_lt, None, Bs(mska), Bs(msk))
        TT("dve", msk[:, :], msk[:, :], msk2[:, :], ALU.mult, Bs(msk, msk2), Bs(msk))
        TS("dve", msk[:, :], msk[:, :], -1.0, BIG, ALU.add, ALU.mult, Bs(msk), Bs(msk))
        for h in range(4):
            TS("dve", msk2[:, :], esi[:, :], 0.0, 2.0 * SLOPES[h], ALU.min, ALU.mult, Bs(esi), Bs(msk2))
            TT("dve", Esb[:, (h * 4 + s) * 32:(h * 4 + s + 1) * 32], msk2[:, :], msk[:, :], ALU.add,
               Bs(msk, msk2), [], Bs(Esb))
    seli = T("seli", [8, 128], F32)
    IOTA(seli[:, :], [[0, 128]], 0, 1, Bs(seli))
    TS("dve", selp[:, :], seli[:, :], 0.0, None, ALU.is_equal, None, Bs(seli), Bs(selp))
    IOTA(selt[:, :], [[1, 128]], 32, -32, Bs(selt))
    TS("dve", selt2[:, :], selt[:, :], 0.0, None, ALU.is_ge, None, Bs(selt), Bs(selt2))
    TS("dve", sels[:, :], selt[:, :], 32.0, None, ALU.is_lt, None, Bs(selt), Bs(sels))
    TT("dve", sels[:, :], sels[:, :], selt2[:, :], ALU.mult, Bs(sels, selt2), Bs(sels))

    for c in range(2):
        MSET("dve", KTblk[c][64:66, :], 1.0, [], [KTB])
        for i in range(NSL):
            MSET("dve", sK[i][c][64:66, :], 1.0, [], [SL[i]])
    IOTA(qrow[0:1, :].rearrange("p (a r) -> p a r", a=4), [[-128, 4], [0, 128]], 0, 0, Bs(qrow))
    P.op("pool", (lambda e: e.iota(qrow[32:33, :].rearrange("p (a r) -> p a r", a=4), [[0, 4], [-1, 128]],
                                   base=0, channel_multiplier=0, allow_small_or_imprecise_dtypes=True)),
         [], [], Bs(qrow))
    for which in range(2):
        for h in range(4):
            TS("dve", qrowb[32 * which:32 * which + 1, h * 512:(h + 1) * 512], qrow[32 * which:32 * which + 1, :],
               SLOPES[h], None, ALU.mult, None, Bs(qrow), [], Bs(qrowb))
    for c in range(2):
        for which in range(2):
            DMA("sp", QTc[c][64 + which:65 + which, :], qrowb[32 * which:32 * which + 1, :], Bs(qrowb), [],
                B(qrowb), QTB)

    b0 = next_bank()
    TR(PS[b0][:, 0:128], stage[:, :], ident_f[:, :], Bs(stage, ident_f), [PSB[b0]])
    CP("dve", prmT[:, :], PS[b0][:, 0:128], [PSB[b0]], Bs(prmT))
    ACT(siluT[:, :], prmT[:, 32:72], AF.Silu, Bs(prmT), Bs(siluT))
    TS("dve", gsub[:, :], prmT[:, 28:29], 1.0 - LAM_INIT, None, ALU.mult, None, Bs(prmT), Bs(gsub))
    TS("dve", FG[:, :], FG[:, :], 32.0, None, ALU.mult, None, Bs(FG), Bs(FG))
    for i in range(2):
        TT("dve", lamp[:, :], lamt[:, (2 * i) * 64:(2 * i + 1) * 64], lamt[:, (2 * i + 1) * 64:(2 * i + 2) * 64],
           ALU.mult, Bs(lamt), Bs(lamp))
        P.op("dve", (lambda e, i=i: e.reduce_sum(out=lams[:, i:i + 1], in_=lamp[:, :], axis=AX.X)),
             Bs(lamp), [], Bs(lams))
    lame = T("lame", [128, 2], F32)
    ACT(lame[:, :], lams[:, 0:2], AF.Exp, Bs(lams), Bs(lame))
    TT("dve", neg_lam[:, :], lame[:, 1:2], lame[:, 0:1], ALU.subtract, Bs(lame), Bs(neg_lam))
    TS("dve", neg_lam[:, :], neg_lam[:, :], -LAM_INIT, None, ALU.add, None, Bs(neg_lam), Bs(neg_lam))

    for n in range(12):
        si = ada_slots[n] if n < NW else ada_load(n)
        b = next_bank()
        for kc in range(8):
            MM(PS[b][0:5, :], siluT[:, kc * 5:(kc + 1) * 5], wsl[si][:, kc * 512:(kc + 1) * 512],
               kc == 0, kc == 7, Bs(siluT) + [WS[si]], [PSB[b]] if kc == 0 else [], [] if kc == 0 else [PSB[b]])
        DMA("sp", badc[n % 2][0:5, :], b_ada[n * 512:(n + 1) * 512].partition_broadcast(5), [], [BADC[n % 2]],
            BADC[n % 2])
        TT("dve", modS[0:5, n * 512:(n + 1) * 512], PS[b][0:5, :], badc[n % 2][0:5, :], ALU.add,
           [PSB[b], BADC[n % 2]], [], MODB)
    b = next_bank()
    chunks = list(range(0, 8)) + list(range(8, 16)) + list(range(24, 32)) + list(range(32, 40))
    for idx, ch in enumerate(chunks):
        TR(PS[b][:, idx * 5:(idx + 1) * 5], modS[0:5, ch * 128:(ch + 1) * 128], ident_f[0:5, 0:5],
           MODB + Bs(ident_f), [PSB[b]] if idx == 0 else [], [] if idx == 0 else [PSB[b]])
    CP("dve", modT[:, :], PS[b][:, 0:160], [PSB[b]], Bs(modT))
    for which in range(2):
        sc0 = 40 if which == 0 else 120
        for kc in range(8):
            TS("dve", GS[:, which * 40 + kc * 5: which * 40 + kc * 5 + 5], modT[:, sc0 + kc * 5: sc0 + kc * 5 + 5],
               1.0, prmT[:, which * 8 + kc: which * 8 + kc + 1], ALU.add, ALU.mult, Bs(modT, prmT), [], Bs(GS))
    TS("dve", GS[:, :], GS[:, :], 32.0, None, ALU.mult, None, Bs(GS), Bs(GS))

    def SHv(which, kc, s):
        c = (0 if which == 0 else 80) + kc * 5 + s
        return modT[:, c:c + 1]

    def GSv(which, kc, s):
        c = which * 40 + kc * 5 + s
        return GS[:, c:c + 1]

    for grp in range(2):
        sel = selp if grp == 0 else sels
        for gi, (Gt, c0) in enumerate(((G1[grp], 16 * 128), (G2[grp], 40 * 128))):
            for hf in range(2):
                b = next_bank()
                MM(PS[b][:, :], sel[0:5, :], modS[0:5, c0 + hf * 512: c0 + (hf + 1) * 512], True, True,
                   MODB + Bs(sel), [PSB[b]])
                CP("dve", Gt[:, hf * 512:(hf + 1) * 512], PS[b][:, :], [PSB[b]], [], Bs(Gt) + ([GBs] if grp else []))
    cstS = T("cstS", [128, 32], F32)
    b = next_bank()
    for j in range(4):
        TR(PS[b][:, j * 8:(j + 1) * 8], stS[:, j * 128:(j + 1) * 128], ident_f[0:8, 0:8],
           Bs(stS, ident_f), [PSB[b]] if j == 0 else [], [] if j == 0 else [PSB[b]])
    CP("dve", cstS[:, :], PS[b][:, 0:32], [PSB[b]], Bs(cstS))

    bar = T("bar", [128, 1], F32)
    P.op("dve", (lambda e: e.memset(bar[:, :], 1.0)), list(_tb.values()) + MODB, [MISC, PRM, LAMB, CONSTB, GB])

    xbt = [[xb[:, t * D:(t + 1) * D] for t in range(4)],
           [G1[1][:, :], G2[1][:, :], yout[:, :], xb2d[:, :]]]
    junkv = [rl[i][:, :].bitcast(BF16) for i in range(2)]

    def stat(kind, t, j):
        c = (kind * 4 + t) * 3 + j
        return stats[:, c:c + 1], STB[c]

    def norm_stats(xset, t, kind):
        q, Q = stat(kind, t, 0)
        a_, A = stat(kind, t, 1)
        r, R = stat(kind, t, 2)
        i = t % 2
        ACT(junkv[i], xbt[xset][t], AF.Square, XBs[xset][t], [Q, RL[i]], accum=q)
        TS("pool", a_, q, D * EPS, None, ALU.add, None, [Q], [A])
        TT("pool", r, a_, mhalf[:, 0:1], ALU.pow, [A, MISC], [R])

    def norm_copy(xset, t, kind):
        r, R = stat(kind, t, 2)
        ACT(xn[t][:, :], xbt[xset][t], AF.Copy, XBs[xset][t] + [R], [XN[t]], scale=r)

    def norm_tr(NT, which, segs):
        TTn = NT // 128
        banks = {}

        def tr(t):
            bks = [next_bank(), next_bank()]
            banks[t] = bks
            pbs = [PS[bk][:, :].bitcast(BF16) for bk in bks]
            for kc in range(8):
                par, kk = kc % 2, kc // 2
                TR(pbs[par][:, kk * 128:(kk + 1) * 128], xn[t][:, kc * 128:(kc + 1) * 128], identb[:, :],
                   [XN[t], MISC], [PSB[bks[par]]] if kk == 0 else [], [] if kk == 0 else [PSB[bks[par]]])

        def ev(t):
            bks = banks[t]
            pbs = [PS[bk][:, :].bitcast(BF16) for bk in bks]
            for kc in range(8):
                par, kk = kc % 2, kc // 2
                for (s_, c0, ncol) in segs:
                    lo = max(c0, t * 128)
                    hi = min(c0 + ncol, (t + 1) * 128)
                    if lo >= hi:
                        continue
                    src = pbs[par][:, kk * 128 + lo - t * 128: kk * 128 + hi - t * 128]
                    dst = hT[kc][:, lo:hi]
                    if par == 0:
                        ACT(dst, src, AF.Identity, [PSB[bks[0]], PRM], [], [HT[kc]],
                            bias=SHv(which, kc, s_), scale=GSv(which, kc, s_))
                    else:
                        TS("dve", dst, src, GSv(which, kc, s_), SHv(which, kc, s_), ALU.mult, ALU.add,
                           [PSB[bks[1]], PRM], [], [HT[kc]])

        tr(0)
        for t in range(TTn):
            if t + 1 < TTn:
                tr(t + 1)
            ev(t)

    unit_rr = [0]

    def attn_unit(h, c0, nq, ktiles, pending=None):
        qt = [QTc[c][0:66, h * 512 + c0: h * 512 + c0 + nq] for c in range(2)]
        sb = [[0, 1], [2, 3]]
        up = unit_rr[0] % 2
        unit_rr[0] += 1
        ob = [4 + 2 * up, 5 + 2 * up]
        if up == 0:
            zacc = [uS[0][:, 0:nq], uS[1][:, 0:nq]]
            ZB = [US[0], US[1]]
        else:
            zacc = [rl[0][:, 0:nq], rl[1][:, 0:nq]]
            ZB = [RL[0], RL[1]]
        n = len(ktiles)
        first = [True, True]

        def emit_S(ti):
            kt = ktiles[ti]
            nk, qlo = kt["nk"], kt["qlo"]
            for c in range(2):
                b = sb[ti % 2][c]
                MM(PS[b][0:nk, qlo:nq], kt["KT"][c], qt[c][:, qlo:nq],
                   True, kt["E"] is None, kt["reads"] + [QTB[h]], [PSB[b]])
                if kt["E"] is not None:
                    Eap, ecol, ew = kt["E"]
                    MM(PS[b][0:nk, ecol:ecol + ew], identb[0:nk, 0:nk], Eap, False, True,
                       [MISC], [], [PSB[b]])

        def emit_rest(ti):
            kt = ktiles[ti]
            nk, qlo = kt["nk"], kt["qlo"]
            assert nk == 128
            for c in range(2):
                b = sb[ti % 2][c]
                pi = (2 * ti + c) % 4
                ACT(pT[pi][0:nk, qlo:nq], PS[b][0:nk, qlo:nq], AF.Exp, [PSB[b], MISC], [PTB[pi]],
                    bias=kt["bias"])
                st = first[c]
                first[c] = False
                MM(PS[ob[c]][:, qlo:nq], kt["V"], pT[pi][0:nk, qlo:nq], st, ti == n - 1,
                   kt["vreads"] + [PTB[pi]], [PSB[ob[c]]] if st else [], [] if st else [PSB[ob[c]]])
                if st:
                    assert qlo == 0
                    CP("dve", zacc[c], pT[pi][:, 0:nq], [PTB[pi]], [ZB[c]])
                else:
                    TT("dve", zacc[c][:, qlo:nq], zacc[c][:, qlo:nq], pT[pi][:, qlo:nq], ALU.add,
                       [PTB[pi], ZB[c]], [ZB[c]])

        emit_S(0)
        for ti in range(n):
            if ti + 1 < n:
                emit_S(ti + 1)
            emit_rest(ti)
            if "after" in ktiles[ti]:
                ktiles[ti]["after"]()
            if pending is not None and ti == min(2, n - 1):
                pending[0](ti % 2)
            if pending is not None and ti == min(6, n - 1):
                pending[1](ti % 2)
                pending = None

        def fin_a(fp):
            _finalize_a(nq, sb[fp], ob, zacc, ZB)

        def fin_b(fp):
            _finalize_b(h, c0, nq, sb[fp])
        return (fin_a, fin_b)

    def _finalize_a(nq, fb, ob, zacc, ZB):
        r0, r1 = cvt[0][:, 0:nq], cvt[1][:, 0:nq]
        a2, a3 = uc[0][:, 0:nq], uc[1][:, 0:nq]
        for c in range(2):
            zbk = fb[c]
            MM(PS[zbk][:, 0:nq], ones_f[:, :], zacc[c], True, True, [ZB[c], MISC], [PSB[zbk]])
            ACT((r0, r1)[c], PS[zbk][:, 0:nq], AF.Ln, [PSB[zbk]], [CVT[c]])
        for c in range(2):
            ACT((r0, r1)[c], (r0, r1)[c], AF.Exp, [CVT[c]], [CVT[c]], scale=-1.0)
        TT("dve", a2, PS[ob[0]][:, 0:nq], r0, ALU.mult, [PSB[ob[0]], CVT[0]], [UC[0]])
        TT("dve", a3, PS[ob[1]][:, 0:nq], r1, ALU.mult, [PSB[ob[1]], CVT[1]], [UC[1]])
        a4 = r0
        STT(a4, a3, neg_lam[:, :], a2, ALU.mult, ALU.add, [UC[0], UC[1], LAMB], [CVT[0]])
        TT("pool", sqb[:, 0:nq], a4, a4, ALU.mult, [CVT[0]], [SQB])

    def _finalize_b(h, c0, nq, fb):
        r1 = cvt[1][:, 0:nq]
        a2 = uc[0][:, 0:nq]
        a4 = cvt[0][:, 0:nq]
        b = fb[0]
        MM(PS[b][:, 0:nq], onesm_b[:, :], sqb[:, 0:nq], True, True, [SQB, MISC], [PSB[b]])
        ACT(r1, PS[b][:, 0:nq], AF.Ln, [PSB[b], MISC], [CVT[1]], bias=epsT[:, :])
        ACT(a2, r1, AF.Exp, [CVT[1]], [UC[0]], scale=-0.5)
        STT(mixT[4 + h][:, c0:c0 + nq], a4, gsub[:, :], a2, ALU.mult, ALU.mult, [CVT[0], UC[0], PRM],
            [], [MIX[4 + h]])

    def mk_ctx(is_sample, bi):
        c = {}
        if is_sample:
            c.update(NT=NSS * DEC_SEQ, NSEG=NSS, L=DEC_SEQ, xsrc=xs, ysrc=y_s, ksrc=k_s, vsrc=v_s, r0=0,
                     segs=[(1 + s_, s_ * DEC_SEQ, DEC_SEQ) for s_ in range(NSS)], grp=1, xset=0)
        else:
            c.update(NT=512, NSEG=1, L=512, xsrc=xp, ysrc=y_p, ksrc=k_p, vsrc=v_p, r0=bi * 512,
                     segs=[(0, 0, 512)], grp=0, xset=(bi + 1) % 2)
        c["is_sample"] = is_sample
        c["bi"] = bi
        return c

    def phase_load(c):
        xset = c["xset"]
        for t in range(c["NT"] // 128):
            extra = [GBs] if (xset == 1 and t < 2) else []
            DMA("sp", xbt[xset][t], c["xsrc"][c["r0"] + t * 128: c["r0"] + (t + 1) * 128, :], [],
                XBs[xset][t] + extra, XBs[xset][t][0])

    def phase_norm1a(c):
        for t in range(c["NT"] // 128):
            norm_stats(c["xset"], t, 0)
        for t in range(c["NT"] // 128):
            norm_copy(c["xset"], t, 0)

    def phase_norm1b(c):
        norm_tr(c["NT"], 0, c["segs"])

    def phase_main(cx, pre=None):
        is_sample, bi, NT, NSEG, L, grp, xset = (cx["is_sample"], cx["bi"], cx["NT"], cx["NSEG"], cx["L"], cx["grp"],
                                                 cx["xset"])
        ksrc, vsrc, r0 = cx["ksrc"], cx["vsrc"], cx["r0"]
        GBr = GBs if is_sample else GB
        TTn = NT // 128
        LP = L + 2
        win3 = wb_in_m.rearrange("(k p) n -> p k n", p=128)
        su = wload(win3[:, :, 1024:1536], 8, 512, WC[0])
        sc_ = wload(win3[:, :, 512:1024], 8, 512, WC[0])
        sB = wload(win3[:, :, 0:512], 8, 512, WC[0])
        for j in range(4):
            i2 = j % 2
            bu, bc, bb = next_bank(), next_bank(), next_bank()
            for (slot, b) in ((su, bu), (sc_, bc), (sB, bb)):
                for kc in range(8):
                    MM(PS[b][:, 0:NT], wsl[slot][:, kc * 512 + j * 128: kc * 512 + (j + 1) * 128], hT[kc][:, 0:NT],
                       kc == 0, kc == 7, [WS[slot], HT[kc]], [PSB[b]] if kc == 0 else [],
                       [] if kc == 0 else [PSB[b]])
            ACT(uS[i2][:, 0:NT], PS[bu][:, 0:NT], AF.Copy, [PSB[bu]], [US[i2]])
            uc3 = uc[i2][:, 0:NSEG * LP].rearrange("p (s l) -> p s l", s=NSEG)
            if is_sample:
                st3 = cstS[:, j * 8:(j + 1) * 8].rearrange("p (s r) -> p s r", s=NSEG)
                CP("pool", uc3[:, :, 0:2], st3, [PRM], [UC[i2]])
            else:
                st3 = cst[:, j * 8: j * 8 + 2].rearrange("p (s r) -> p s r", s=1)
                CP("pool", uc3[:, :, 0:2], st3, [CST[j]], [UC[i2]])
            TT("dve", uc3[:, :, 2:LP], PS[bc][:, 0:NT].rearrange("p (s l) -> p s l", s=NSEG),
               uS[i2][:, 0:NT].rearrange("p (s l) -> p s l", s=NSEG), ALU.mult,
               [PSB[bc], US[i2]], [], [UC[i2]])
            if is_sample:
                CP("pool", cfin[:, j * 8:(j + 1) * 8].rearrange("p (s r) -> p s r", s=NSEG), uc3[:, :, L:LP],
                   [UC[i2]], [], [CFIN])
            else:
                CP("pool", cst[:, j * 8: j * 8 + 2].rearrange("p (s r) -> p s r", s=1), uc3[:, :, L:LP],
                   [UC[i2]], [CST[j]])
            cv3 = cvt[i2][:, 0:NT].rearrange("p (s l) -> p s l", s=NSEG)

            def cw(tap, j=j):
                return prmT[:, 16 + tap * 4 + j: 16 + tap * 4 + j + 1]
            TS("dve", cv3, uc3[:, :, 2:LP], cw(2), None, ALU.mult, None, [UC[i2], PRM], [CVT[i2]])
            for tap in (1, 0):
                def fn(e, tap=tap, cv3=cv3, uc3=uc3, cw=cw):
                    return e.scalar_tensor_tensor(out=cv3, in0=uc3[:, :, tap:tap + L], scalar=cw(tap), in1=cv3,
                                                  op0=ALU.mult, op1=ALU.add)
                P.op("dve", fn, [UC[i2], PRM, CVT[i2]], [CVT[i2]])
            TT("dve", mixT[j][:, 0:NT], PS[bb][:, 0:NT], cvt[i2][:, 0:NT], ALU.mult, [PSB[bb], CVT[i2]], [MIX[j]])
        _chk(12)
        sq_ = wload(win3[:, :, 1536:2048], 8, 512, WC[0])
        for h in range(4):
            for c in range(2):
                b = next_bank()
                for kc in range(8):
                    w0 = kc * 512 + h * 128 + c * 64
                    MM(PS[b][0:64, 0:NT], wsl[sq_][:, w0:w0 + 64], hT[kc][:, 0:NT],
                       kc == 0, kc == 7, [WS[sq_], HT[kc]], [PSB[b]] if kc == 0 else [], [] if kc == 0 else [PSB[b]])
                ACT(QTc[c][0:64, h * 512: h * 512 + NT], PS[b][0:64, 0:NT], AF.Copy, [PSB[b]], [], [QTB[h]],
                    scale=0.125)
        _chk(13)
        sk_ = wload(win3[:, :, 2048:2560], 8, 512, WC[0])
        sv_ = wload(win3[:, :, 2560:3072], 8, 512, WC[0])
        for t in range(TTn):
            i2 = t % 2
            b = next_bank()
            for kc in range(8):
                MM(PS[b][:, :], hT[kc][:, t * 128:(t + 1) * 128], wsl[sk_][:, kc * 512:(kc + 1) * 512],
                   kc == 0, kc == 7, [WS[sk_], HT[kc]], [PSB[b]] if kc == 0 else [], [] if kc == 0 else [PSB[b]])
            ACT(kf[i2][:, :], PS[b][:, :], AF.Copy, [PSB[b]], [KF[i2]])
            CP("dve", kb[i2][:, :], kf[i2][:, :], [KF[i2]], [KB[i2]])
            DMA("pool", ksrc[r0 + t * 128: r0 + (t + 1) * 128, :], kf[i2][:, :], [KF[i2]], [], KF[i2])
            bv = next_bank()
            for kc in range(8):
                MM(PS[bv][:, :], hT[kc][:, t * 128:(t + 1) * 128], wsl[sv_][:, kc * 512:(kc + 1) * 512],
                   kc == 0, kc == 7, [WS[sv_], HT[kc]], [PSB[bv]] if kc == 0 else [], [] if kc == 0 else [PSB[bv]])
            for c in range(2):
                b2 = next_bank()
                pb = PS[b2][:, :].bitcast(BF16)
                for h in range(4):
                    hc = h * 2 + c
                    TR(pb[0:64, h * 128:(h + 1) * 128], kb[i2][:, hc * 64:(hc + 1) * 64], identb[:, :],
                       [KB[i2], MISC], [PSB[b2]] if h == 0 else [], [] if h == 0 else [PSB[b2]])
                dst = KTblk[c][0:64, :].rearrange("p (h k) -> p h k", h=4)[:, :, t * 128:(t + 1) * 128]
                srcv = pb[0:64, 0:512].rearrange("p (h k) -> p h k", h=4)
                if c:
                    CP("dve", dst, srcv, [PSB[b2]], [], [KTB])
                else:
                    ACT(dst, srcv, AF.Copy, [PSB[b2]], [], [KTB])
            b = bv
            ACT(vf[i2][:, :], PS[b][:, :], AF.Copy, [PSB[b]], [VF[i2]])
            vdst = Vblk[:, :].rearrange("p (h t d) -> p h t d", h=4, t=4)[:, :, t, :]
            CP("dve", vdst, vf[i2][:, :].rearrange("p (h d) -> p h d", h=4), [VF[i2]], [], [VB])
            DMA("pool", vsrc[r0 + t * 128: r0 + (t + 1) * 128, :], vf[i2][:, :], [VF[i2]], [], VF[i2])
        _chk(14)
        if (not is_sample) and bi < NB - 1:
            for c in range(2):
                DMA("pool", kt_scr[bi * 128 + c * 64: bi * 128 + (c + 1) * 64, :], KTblk[c][0:64, :], [KTB],
                    [KSCR[bi]] if c == 0 else [], KTB, [] if c == 0 else [KSCR[bi]])
            DMA("pool", v_scr[bi * 128:(bi + 1) * 128, :], Vblk[:, :], [VB], [VSCR[bi]], VB)
        if is_sample or bi == NB - 1:
            ns2 = NSEG * 2
            b = next_bank()
            for j in range(4):
                src = cfin[:, j * 8: j * 8 + ns2] if is_sample else cst[:, j * 8: j * 8 + 2]
                TR(PS[b][0:ns2, j * 128:(j + 1) * 128], src, ident_f[:, :],
                   ([CFIN] if is_sample else [CST[j]]) + [MISC], [PSB[b]] if j == 0 else [],
                   [] if j == 0 else [PSB[b]])
            CP("dve", cfo[0:ns2, :], PS[b][0:ns2, :], [PSB[b]], [CFO])
            DMA("pool", (conv_s if is_sample else conv_p), cfo[0:ns2, :], [CFO], [], CFO)

        _chk(1 if is_sample else 101)
        pend = [None]
        if is_sample:
            for s in range(NSS):
                if pend[0] is not None:
                    pend[0][0](0)
                    pend[0][1](0)
                    pend[0] = None
                DMA("pool", Kraw.rearrange("p (t f) -> p t f", t=PT),
                    ck[s * PAST:(s + 1) * PAST, :].rearrange("(t p) f -> p t f", p=128), [], KRB, KRB[0])
                for h in range(4):
                    DMA("pool", Vsm[:, h * PT * 128:(h + 1) * PT * 128].rearrange("p (t d) -> p t d", t=PT),
                        cv[s * PAST:(s + 1) * PAST, h * 128:(h + 1) * 128].rearrange("(t p) d -> p t d", p=128),
                        [], VSB if h == 0 else [], VSB[0], [] if h == 0 else VSB)
                for c in range(2):
                    MSET("dve", KTs[c][64:66, :], 1.0, [], KSB)
                for h in range(4):
                    for c in range(2):
                        for t0 in range(0, PT, 8):
                            tn = min(8, PT - t0)
                            b = next_bank()
                            pb = PS[b][:, :].bitcast(BF16)
                            for tt in range(tn):
                                t = t0 + tt
                                k0 = t * 512 + h * 128 + c * 64
                                TR(pb[0:64, tt * 128:(tt + 1) * 128], Kraw[:, k0:k0 + 64],
                                   identb[:, :], KRB + [MISC], [PSB[b]] if tt == 0 else [],
                                   [] if tt == 0 else [PSB[b]])
                            CP("dve", KTs[c][0:64, h * PT * 128 + t0 * 128: h * PT * 128 + (t0 + tn) * 128],
                               pb[0:64, 0:tn * 128], [PSB[b]], [], KSB)
                for h in range(4):
                    kts = []
                    for t in range(PT):
                        kts.append(dict(
                            KT=[KTs[c][:, h * PT * 128 + t * 128: h * PT * 128 + (t + 1) * 128] for c in range(2)],
                            V=Vsm[:, (h * PT + t) * 128:(h * PT + t + 1) * 128], nk=128, qlo=0,
                            bias=tab[:, h * NTAB + (PT - t) + TABOFF: h * NTAB + (PT - t) + TABOFF + 1],
                            E=None, reads=KSB, vreads=VSB))
                    kts.append(dict(
                        KT=[KTblk[c][:, h * 512: h * 512 + 128] for c in range(2)],
                        V=Vblk[:, h * 512: h * 512 + 128], nk=128, qlo=0,
                        bias=tabS[:, h * 4 + s: h * 4 + s + 1],
                        E=(Esb[:, (h * 4 + s) * 32:(h * 4 + s + 1) * 32], 0, 32), reads=[KTB], vreads=[VB]))
                    pend[0] = attn_unit(h, s * DEC_SEQ, DEC_SEQ, kts, pend[0])
        else:
            for h in range(4):
                kts = []
                base = sl_rr[0]
                sl_rr[0] = (base + bi) % NSL

                def fill(bprev, h=h, base=base):
                    si = (base + bprev) % NSL
                    for c in range(2):
                        DMA("sp", sK[si][c][0:64, :],
                            kt_scr[bprev * 128 + c * 64: bprev * 128 + (c + 1) * 64, h * 512:(h + 1) * 512],
                            [KSCR[bprev]], [SL[si]] if c == 0 else [], SL[si], [] if c == 0 else [SL[si]])
                    DMA("sp", sV[si][:, :], v_scr[bprev * 128:(bprev + 1) * 128, h * 512:(h + 1) * 512],
                        [VSCR[bprev]], [], SL[si], [SL[si]])

                for bprev in range(bi):
                    si = (base + bprev) % NSL
                    for t in range(4):
                        jj = 4 * (bi - bprev) - t
                        kt = dict(
                            KT=[sK[si][c][:, t * 128:(t + 1) * 128] for c in range(2)],
                            V=sV[si][:, t * 128:(t + 1) * 128], nk=128, qlo=0,
                            bias=tab[:, h * NTAB + jj + TABOFF: h * NTAB + jj + TABOFF + 1],
                            E=None, reads=[SL[si]], vreads=[SL[si]])
                        if t == 3 and bprev + NSL < bi:
                            kt["after"] = (lambda b=bprev + NSL, fill=fill: fill(b))
                        kts.append(kt)
                for bprev in range(min(NSL, bi)):
                    fill(bprev)
                for t in range(4):
                    kts.append(dict(
                        KT=[KTblk[c][:, h * 512 + t * 128: h * 512 + (t + 1) * 128] for c in range(2)],
                        V=Vblk[:, (h * 4 + t) * 128:(h * 4 + t + 1) * 128], nk=128, qlo=t * 128,
                        bias=tab[:, h * NTAB - t + TABOFF: h * NTAB - t + TABOFF + 1],
                        E=(E0b[:, h * 128:(h + 1) * 128], t * 128, 128), reads=[KTB], vreads=[VB]))
                pend[0] = attn_unit(h, 0, 512, kts, pend[0])
                if h == 0 and pre is not None:
                    phase_load(pre)

        if pend[0] is not None:
            pend[0][0](0)
            pend[0][1](0)
        _chk(2 if is_sample else 102)
        wo3 = wb_o_m.rearrange("(k p) n -> p k n", p=128)
        sos = [wload(wo3[:, :, hf * 512:(hf + 1) * 512], 8, 512, WC[1]) for hf in range(2)]
        for t in range(TTn):
            for hf in range(2):
                so = sos[hf]
                b = next_bank()
                for kc in range(8):
                    MM(PS[b][:, :], mixT[kc][:, t * 128:(t + 1) * 128], wsl[so][:, kc * 512:(kc + 1) * 512],
                       kc == 0, kc == 7, [WS[so], MIX[kc]], [PSB[b]] if kc == 0 else [], [] if kc == 0 else [PSB[b]])
                TT("dve", tmpr[:, :], PS[b][:, :], G1[grp][:, hf * 512:(hf + 1) * 512], ALU.mult,
                   [PSB[b], GBr], [TMPR])
                xs_ = xbt[xset][t][:, hf * 512:(hf + 1) * 512]
                TT("dve", xs_, xs_, tmpr[:, :], ALU.add, [TMPR, XBs[xset][t][hf]], [XBs[xset][t][hf]])
            norm_stats(xset, t, 1)
            if t >= 1:
                norm_copy(xset, t - 1, 1)
        norm_copy(xset, TTn - 1, 1)
        norm_tr(NT, 1, cx["segs"])
        if pre is not None:
            phase_norm1a(pre)
        _chk(3 if is_sample else 103)
        w13 = wb_1_m.rearrange("(k p) n -> p k n", p=128)
        for g in range(8):
            s1 = wload(w13[:, :, g * 512:(g + 1) * 512], 8, 512, WC[2])
            for j in range(4):
                jj = g * 4 + j
                i2 = jj % 2
                b = next_bank()
                for kc in range(8):
                    MM(PS[b][:, 0:NT], wsl[s1][:, kc * 512 + j * 128: kc * 512 + (j + 1) * 128], hT[kc][:, 0:NT],
                       kc == 0, kc == 7, [WS[s1], HT[kc]], [PSB[b]] if kc == 0 else [], [] if kc == 0 else [PSB[b]])
                ACT(rl[i2][:, 0:NT], PS[b][:, 0:NT], AF.Relu, [PSB[b]], [RL[i2]])
                TT("pool", HID[:, jj * 512: jj * 512 + NT], rl[i2][:, 0:NT], rl[i2][:, 0:NT], ALU.mult,
                   [RL[i2]], [HB[jj]])
    def phase_tail(c, pre=None):
        is_sample, NT, grp, xset, ysrc, r0 = c["is_sample"], c["NT"], c["grp"], c["xset"], c["ysrc"], c["r0"]
        GBr = GBs if is_sample else GB
        TTn = NT // 128
        if pre is not None:
            phase_norm1b(pre)
        w23 = wb_2_m.rearrange("(j p) n -> p j n", p=128)
        for g in range(8):
            s2 = wload(w23[:, g * 4:(g + 1) * 4, :], 4, 1024, WC[3])
            for j in range(4):
                jj = g * 4 + j
                for t in range(TTn):
                    for hf in range(2):
                        b = t * 2 + hf
                        MM(PS[b][:, :], HID[:, jj * 512 + t * 128: jj * 512 + (t + 1) * 128],
                           wsl[s2][:, j * 1024 + hf * 512: j * 1024 + (hf + 1) * 512],
                           jj == 0, jj == 31, [WS[s2], HB[jj]], [PSB[b]] if jj == 0 else [],
                           [] if jj == 0 else [PSB[b]])
        ps_rr[0] = 0

        def fin_out(t):
            xt = xbt[xset][t]
            r, R = stat(2, t, 2)
            STT(xt, xt, r, FG[:, :], ALU.mult, ALU.mult, XBs[xset][t] + [R, LAMB], XBs[xset][t])
            DMA("pool", ysrc[r0 + t * 128: r0 + (t + 1) * 128, :], xt, XBs[xset][t], [], XBs[xset][t][1])

        for t in range(TTn):
            for hf in range(2):
                b = t * 2 + hf
                TT("dve", tmpr[:, :], PS[b][:, :], G2[grp][:, hf * 512:(hf + 1) * 512], ALU.mult,
                   [PSB[b], GBr], [TMPR])
                xs_ = xbt[xset][t][:, hf * 512:(hf + 1) * 512]
                TT("dve", xs_, xs_, tmpr[:, :], ALU.add, [TMPR, XBs[xset][t][hf]], [XBs[xset][t][hf]])
            norm_stats(xset, t, 2)
            if t >= 1:
                fin_out(t - 1)
        fin_out(TTn - 1)

    KSCR = [P.buf("kscr%d" % i) for i in range(NB)]
    VSCR = [P.buf("vscr%d" % i) for i in range(NB)]
    sl_rr = [0]

    try:
        _chk(0)
        ctxs = [mk_ctx(True, 0)] + [mk_ctx(False, bi) for bi in range(NB)]
        phase_load(ctxs[0])
        phase_norm1a(ctxs[0])
        phase_norm1b(ctxs[0])
        for i, c in enumerate(ctxs):
            nxt = ctxs[i + 1] if i + 1 < len(ctxs) else None
            if nxt is not None and i >= 1:
                phase_main(c, nxt)
                phase_tail(c, nxt)
            elif nxt is not None:
                phase_main(c)
                phase_tail(c)
                phase_load(nxt)
                phase_norm1a(nxt)
                phase_norm1b(nxt)
            else:
                phase_main(c)
                phase_tail(c)
            _chk(4 + i)
    except _Stop:
        pass

    P.emit()
    stack.close()
    return nc


_CACHE = {}


def kernel(x_prompt, x_sample, cache_k, cache_v, state_conv, c_prompt, c_sample,
           norm1_g, norm2_g, w_ada, b_ada, w_in, conv_w, lambda_q1, lambda_k1, lambda_q2, lambda_k2,
           subln_g, w_o, w_mlp1, w_mlp2, final_g):
    f = lambda a: np.ascontiguousarray(np.asarray(a), dtype=np.float32)
    x_prompt, x_sample, cache_k, cache_v, state_conv = map(f, (x_prompt, x_sample, cache_k, cache_v, state_conv))
    c_prompt, c_sample = f(c_prompt), f(c_sample)
    NCORE, SEQ, _ = x_prompt.shape
    DECB, TS_, _ = x_sample.shape
    PAST = cache_k.shape[2]
    assert DECB == NSS * NCORE and TS_ == DEC_SEQ
    key = (SEQ, PAST)
    if key not in _CACHE:
        _CACHE[key] = build(SEQ, PAST)
    nc = _CACHE[key]
    shared = {
        "norm1_g": f(norm1_g).reshape(8, 128), "norm2_g": f(norm2_g).reshape(8, 128),
        "w_ada": f(w_ada).reshape(D, 6 * D), "b_ada": f(b_ada).reshape(6 * D),
        "w_in": f(w_in).reshape(768, 4096), "conv_w": f(conv_w).reshape(12, 128),
        "lq1": f(lambda_q1).reshape(64), "lk1": f(lambda_k1).reshape(64),
        "lq2": f(lambda_q2).reshape(64), "lk2": f(lambda_k2).reshape(64),
        "subln_g": f(subln_g).reshape(1, 128), "w_o": f(w_o).reshape(256, 4096),
        "w_mlp1": f(w_mlp1).reshape(1024, 4096), "w_mlp2": f(w_mlp2).reshape(1024, 4096),
        "final_g": f(final_g).reshape(D),
    }
    in_maps = []
    for c in range(NCORE):
        sl = slice(NSS * c, NSS * (c + 1))
        m = dict(shared)
        m["xp"] = x_prompt[c]
        m["xs"] = x_sample[sl].reshape(NSS * DEC_SEQ, D)
        m["ck"] = cache_k[0, sl].reshape(NSS * PAST, 512)
        m["cv"] = cache_v[0, sl].reshape(NSS * PAST, 512)
        m["sconv"] = state_conv[0, sl].reshape(NSS * 2, 512)
        m["cc"] = np.ascontiguousarray(np.concatenate([c_prompt[c:c + 1], c_sample[sl]], axis=0))
        in_maps.append(m)
    res = run_bass_kernel_spmd(nc, in_maps, core_ids=list(range(NCORE)))
    R = res.results
    g = lambda name: np.stack([np.asarray(R[c][name], dtype=np.float32) for c in range(NCORE)], axis=0)
    y_prompt = g("y_p")
    y_sample = g("y_s").reshape(DECB, DEC_SEQ, D)
    k_prompt = g("k_p").reshape(1, NCORE, SEQ, NH, 128)
    v_prompt = g("v_p").reshape(1, NCORE, SEQ, NH, 128)
    conv_prompt = g("conv_p").reshape(1, NCORE, 2, CONV_CH)
    k_sample = g("k_s").reshape(1, DECB, DEC_SEQ, NH, 128)
    v_sample = g("v_s").reshape(1, DECB, DEC_SEQ, NH, 128)
    conv_sample = g("conv_s").reshape(1, DECB, 2, CONV_CH)
    return (y_prompt, y_sample, k_prompt, v_prompt, conv_prompt, k_sample, v_sample, conv_sample)
```

```python
import math
from contextlib import ExitStack

import numpy as np
import concourse.bass as bass
import concourse.mybir as mybir
from concourse.bass_utils import run_bass_kernel_spmd

F32 = mybir.dt.float32
BF16 = mybir.dt.bfloat16
ALU = mybir.AluOpType
AF = mybir.ActivationFunctionType
AX = mybir.AxisListType

EPS = 1e-5
D = 1024
NH = 4
CONV_CH = 512
DFF = 4096
SLOPES = [2.0 ** (-8.0 * (i + 1) / NH) for i in range(NH)]
LAM_INIT = 0.8 - 0.6 * math.exp(-0.3 * 0)
BIG = 30000.0
DEC_SEQ = 32
NSS = 4
TABOFF = 3
DBG = {"stop": None}


class _Stop(Exception):
    pass


def _chk(n):
    if DBG["stop"] is not None and DBG["stop"] == n:
        raise _Stop()


class Buf:
    __slots__ = ("name", "writers", "readers", "prev_readers", "dsem", "dcount", "psum")

    def __init__(self, name):
        self.name = name
        self.psum = False
        self.writers = []
        self.readers = []
        self.prev_readers = []
        self.dsem = None
        self.dcount = 0


class Op:
    __slots__ = ("eng", "fn", "deps", "is_dma", "sem", "val", "signal", "idx")


class Prog:
    ENGS = ("sp", "pool", "act", "dve", "pe")

    def __init__(self, nc, stack):
        self.nc = nc
        self.stack = stack
        self.ops = {e: [] for e in self.ENGS}
        self.all = []
        self.dma_bufs = []
        self.eng_sem = {e: stack.enter_context(nc.semaphore("es_" + e)) for e in self.ENGS}

    def buf(self, name):
        return Buf(name)

    def op(self, eng, fn, reads=(), writes=(), pwrites=(), dma_buf=None):
        o = Op()
        o.eng = eng
        o.fn = fn
        o.is_dma = dma_buf is not None
        o.deps = {}
        o.signal = False
        o.sem = None
        o.val = 0

        def add(d, raw):
            if d is o:
                return
            need = d.is_dma or o.is_dma or (d.eng != eng) or raw or eng != "pe"
            o.deps[d] = o.deps.get(d, False) or need

        for b in reads:
            for w in b.writers:
                add(w, True)
            if b.psum:
                for r in b.readers:
                    if r.eng != eng:
                        add(r, True)
        for b in writes:
            if b.readers:
                for r in b.readers:
                    add(r, False)
            else:
                for w in b.writers:
                    add(w, False)
                for r in b.prev_readers:
                    add(r, False)
        for b in pwrites:
            if b.readers:
                for r in b.readers:
                    add(r, False)
            else:
                for r in b.prev_readers:
                    add(r, False)
        for b in reads:
            b.readers.append(o)
        for b in writes:
            if b.readers:
                b.prev_readers = b.readers
                b.readers = []
            b.writers = [o]
        for b in pwrites:
            if b.readers:
                b.prev_readers = b.readers
                b.readers = []
                b.writers = [o]
            else:
                b.writers.append(o)
        if o.is_dma:
            if dma_buf.dsem is None:
                dma_buf.dsem = {}
            if eng not in dma_buf.dsem:
                sem = self.stack.enter_context(self.nc.semaphore("ds_%s_%s" % (dma_buf.name, eng)))
                dma_buf.dsem[eng] = [sem, 0]
                self.dma_bufs.append(dma_buf.dsem[eng])
            ent = dma_buf.dsem[eng]
            ent[1] += 1
            o.sem = ent[0]
            o.val = 16 * ent[1]
        self.ops[eng].append(o)
        self.all.append(o)
        return o

    def emit(self):
        nc = self.nc
        for e in self.ENGS:
            for i, o in enumerate(self.ops[e]):
                o.idx = i
        for o in self.all:
            last = {}
            nd = {}
            for d, need in o.deps.items():
                if not need:
                    continue
                if d.is_dma:
                    nd[d] = True
                else:
                    if d.eng not in last or last[d.eng].idx < d.idx:
                        last[d.eng] = d
            for d in last.values():
                nd[d] = True
            o.deps = nd
            for d in nd:
                d.signal = True
        for e in self.ENGS:
            cnt = 0
            for o in self.ops[e]:
                if (not o.is_dma) and o.signal:
                    cnt += 1
                    o.val = cnt
                    o.sem = self.eng_sem[e]

        def run(e, eng):
            waited = {}
            for o in self.ops[e]:
                need_w = {}
                for d, need in o.deps.items():
                    if not need:
                        continue
                    k = id(d.sem)
                    if k not in need_w or need_w[k][1] < d.val:
                        need_w[k] = (d.sem, d.val)
                for k, (sem, val) in need_w.items():
                    if waited.get(k, 0) < val:
                        eng.wait_ge(sem, val)
                        waited[k] = val
                ins = o.fn(eng)
                if o.is_dma:
                    ins.then_inc(o.sem, 16)
                elif o.signal:
                    ins.then_inc(o.sem, 1)
            if e == "pool":
                for (sem, cnt) in self.dma_bufs:
                    eng.wait_ge(sem, 16 * cnt)

        with nc.Block() as block:
            @block.sync
            def _(eng):
                run("sp", eng)

            @block.gpsimd
            def _(eng):
                run("pool", eng)

            @block.scalar
            def _(eng):
                run("act", eng)

            @block.vector
            def _(eng):
                run("dve", eng)

            @block.tensor
            def _(eng):
                run("pe", eng)


def build(SEQ, PAST):
    NB = SEQ // 512
    PT = PAST // 128
    assert SEQ % 512 == 0 and PAST % 128 == 0
    NTAB = max(4 * NB, PT) + TABOFF + 1

    nc = bass.Bass("TRN2", target_bir_lowering=False)
    stack = ExitStack()
    P = Prog(nc, stack)

    def din(name, shape, dt=F32):
        return nc.dram_tensor(name, shape, dt, kind="ExternalInput").ap()

    def dout(name, shape, dt=F32):
        return nc.dram_tensor(name, shape, dt, kind="ExternalOutput").ap()

    def dscr(name, shape, dt):
        return nc.dram_tensor(name, shape, dt, kind="Internal").ap()

    xp = din("xp", [SEQ, D])
    xs = din("xs", [NSS * DEC_SEQ, D])
    ck = din("ck", [NSS * PAST, 512])
    cv = din("cv", [NSS * PAST, 512])
    sconv = din("sconv", [NSS * 2, 512])
    cc = din("cc", [1 + NSS, D])
    norm1_g = din("norm1_g", [8, 128])
    norm2_g = din("norm2_g", [8, 128])
    w_ada = din("w_ada", [D, 6 * D])
    b_ada = din("b_ada", [6 * D])
    w_in = din("w_in", [768, 4096])
    conv_w = din("conv_w", [12, 128])
    lam_in = [din("lq1", [64]), din("lk1", [64]), din("lq2", [64]), din("lk2", [64])]
    subln_g = din("subln_g", [1, 128])
    w_o = din("w_o", [256, 4096])
    w_mlp1 = din("w_mlp1", [1024, 4096])
    w_mlp2 = din("w_mlp2", [1024, 4096])
    final_g = din("final_g", [D])

    y_p = dout("y_p", [SEQ, D])
    y_s = dout("y_s", [NSS * DEC_SEQ, D])
    k_p = dout("k_p", [SEQ, 512])
    v_p = dout("v_p", [SEQ, 512])
    conv_p = dout("conv_p", [2, 512])
    k_s = dout("k_s", [NSS * DEC_SEQ, 512])
    v_s = dout("v_s", [NSS * DEC_SEQ, 512])
    conv_s = dout("conv_s", [NSS * 2, 512])

    wb_in = dscr("wb_in", [768, 4096], BF16)
    wb_o = dscr("wb_o", [256, 4096], BF16)
    wb_1 = dscr("wb_1", [1024, 4096], BF16)
    wb_2 = dscr("wb_2", [1024, 4096], BF16)
    kt_scr = dscr("kt_scr", [NB * 128, 2048], BF16)
    v_scr = dscr("v_scr", [NB * 128, 2048], BF16)
    wb_in_m = wb_in.rearrange("a b -> (a b)").rearrange("(r n) -> r n", n=3072)
    wb_o_m = wb_o.rearrange("a b -> (a b)").rearrange("(r n) -> r n", n=1024)
    wb_1_m = wb_1
    wb_2_m = wb_2.rearrange("a b -> (a b)").rearrange("(r n) -> r n", n=1024)

    def T(name, shape, dt):
        return stack.enter_context(nc.sbuf_tensor(name, shape, dt))

    PS = [stack.enter_context(nc.psum_tensor("ps%d" % i, [128, 512], F32)) for i in range(8)]
    PSB = [P.buf("ps%d" % i) for i in range(8)]
    for _b in PSB:
        _b.psum = True
    ps_rr = [0]

    def next_bank():
        i = ps_rr[0]
        ps_rr[0] = (i + 1) % 8
        return i

    xb = T("xb", [128, 4 * D], F32)
    xb2d = T("xb2d", [128, D], F32)
    XBs = [[[P.buf("xb%d_%d_%d" % (st_, t, hf)) for hf in range(2)] for t in range(4)] for st_ in range(2)]
    stats = T("stats", [128, 64], F32)
    STB = [P.buf("stat%d" % i) for i in range(64)]
    xn = [T("xn%d" % i, [128, D], BF16) for i in range(4)]
    XN = [P.buf("xn%d" % i) for i in range(4)]
    hT = [T("hT%d" % i, [128, 512], BF16) for i in range(8)]
    HT = [P.buf("hT%d" % i) for i in range(8)]
    uS = [T("uS%d" % i, [128, 512], F32) for i in range(2)]
    US = [P.buf("uS%d" % i) for i in range(2)]
    uc = [T("uc%d" % i, [128, 520], F32) for i in range(2)]
    UC = [P.buf("uc%d" % i) for i in range(2)]
    cvt = [T("cvt%d" % i, [128, 512], F32) for i in range(2)]
    CVT = [P.buf("cvt%d" % i) for i in range(2)]
    cst = T("cst", [128, 4 * 8], F32)
    CST = [P.buf("cst%d" % j) for j in range(4)]
    QTc = [T("QT%d" % c, [66, 4 * 512], BF16) for c in range(2)]
    QTB = [P.buf("QT%d" % h) for h in range(4)]
    KTblk = [T("KTblk%d" % c, [66, 4 * 512], BF16) for c in range(2)]
    KTB = P.buf("KTblk")
    Vblk = T("Vblk", [128, 4 * 512], BF16)
    VB = P.buf("Vblk")
    kf = [T("kf%d" % i, [128, 512], F32) for i in range(2)]
    KF = [P.buf("kf%d" % i) for i in range(2)]
    vf = [T("vf%d" % i, [128, 512], F32) for i in range(2)]
    VF = [P.buf("vf%d" % i) for i in range(2)]
    qrow = kf[0]
    qrowb = Vblk
    kb = [T("kb%d" % i, [128, 512], BF16) for i in range(2)]
    KB = [P.buf("kb%d" % i) for i in range(2)]
    mixT = [T("mixT%d" % i, [128, 512], BF16) for i in range(8)]
    MIX = [P.buf("mixT%d" % i) for i in range(8)]
    pT = [T("pT%d" % i, [128, 512], BF16) for i in range(4)]
    PTB = [P.buf("pT%d" % i) for i in range(4)]
    NSL = 4
    sK = [[T("sK%d_%d" % (i, c), [66, 512], BF16) for c in range(2)] for i in range(NSL)]
    sV = [T("sV%d" % i, [128, 512], BF16) for i in range(NSL)]
    SL = [P.buf("sl%d" % i) for i in range(NSL)]
    at = [uS[0], uS[1], cvt[0], cvt[1], uc[0]]
    AT = [US[0], US[1], CVT[0], CVT[1], UC[0]]
    sqb = T("sqb", [128, 512], BF16)
    SQB = P.buf("sqb")
    HID = T("HID", [128, 32 * 512], BF16)
    HB = [P.buf("hid%d" % i) for i in range(32)]
    G1 = [T("G1_%d" % i, [128, D], F32) for i in range(2)]
    G2 = [T("G2_%d" % i, [128, D], F32) for i in range(2)]
    GB = P.buf("gates")
    GBs = P.buf("gates_s")
    FG = T("FG", [128, D], F32)
    tmpr = T("tmpr", [128, 512], F32)
    TMPR = P.buf("tmpr")
    cfo = tmpr
    stS = tmpr[0:8, :]
    rl = [T("rl%d" % i, [128, 512], F32) for i in range(2)]
    RL = [P.buf("rl%d" % i) for i in range(2)]
    yout = T("yout", [128, D], F32)
    YOUT = P.buf("yout")
    NW = 3
    wsl = [T("wsl%d" % i, [128, 4096], BF16) for i in range(NW)]
    WS = [P.buf("wsl%d" % i) for i in range(NW)]
    w_rr = [0]

    ident_f = T("ident_f", [128, 128], F32)
    identb = T("identb", [128, 128], BF16)
    ones_f = T("ones_f", [128, 128], F32)
    onesm_b = T("onesm_b", [128, 128], BF16)
    mhalf = T("mhalf", [128, 1], F32)
    epsT = T("epsT", [128, 1], F32)
    iot = T("iot", [128, 128], F32)
    tab = T("tab", [128, 4 * NTAB], F32)
    tabS = T("tabS", [128, 16], F32)
    E0b = T("E0b", [128, 4 * 128], BF16)
    Esb = T("Esb", [128, 16 * 32], BF16)
    e0f = T("e0f", [128, 128], F32)
    msk = T("msk", [128, 32], F32)
    msk2 = T("msk2", [128, 32], F32)
    stage = T("stage", [128, 128], F32)
    prmT = T("prmT", [128, 128], F32)
    siluT = T("siluT", [128, 40], BF16)
    modT = T("modT", [128, 160], F32)
    GS = T("GS", [128, 80], F32)
    selp = T("selp", [8, 128], F32)
    sels = T("sels", [8, 128], F32)
    selt = T("selt", [8, 128], F32)
    selt2 = T("selt2", [8, 128], F32)
    lamt = T("lamt", [128, 4 * 64], F32)
    lamp = T("lamp", [128, 64], F32)
    lams = T("lams", [128, 4], F32)
    neg_lam = T("neg_lam", [128, 1], F32)
    gsub = T("gsub", [128, 1], F32)
    badc = [T("badc%d" % i, [8, 512], F32) for i in range(2)]
    BADC = [P.buf("badc%d" % i) for i in range(2)]
    cfin = T("cfin", [128, 32], F32)
    CONSTB = P.buf("const")
    PRM = P.buf("prm")
    LAMB = P.buf("lam")
    MISC = P.buf("misc")
    CFIN = P.buf("cfin")
    CFO = TMPR

    modS = HID[0:8, 0:12288].bitcast(F32)
    MODB = HB[0:24]
    Kraw = HID[:, 0:PT * 512]
    KRB = HB[0:PT]
    Vsm = HID[:, 8 * 512:8 * 512 + PT * 512]
    VSB = HB[8:8 + PT]
    KTs = [HID[0:66, 16 * 512 + c * 4 * PT * 128:16 * 512 + (c + 1) * 4 * PT * 128] for c in range(2)]
    KSB = HB[16:16 + 2 * PT]

    def ACT(out, in_, func, reads, writes=(), pwrites=(), bias=None, scale=None, accum=None):
        def fn(e):
            kw = {}
            if bias is not None:
                kw["bias"] = bias
            if scale is not None:
                kw["scale"] = scale
            if accum is not None:
                kw["accum_out"] = accum
            return e.activation(out=out, in_=in_, func=func, **kw)
        return P.op("act", fn, reads, writes, pwrites)

    def TS(eng, out, in0, s1, s2, op0, op1, reads, writes=(), pwrites=()):
        def fn(e):
            if s2 is None:
                return e.tensor_scalar(out=out, in0=in0, scalar1=s1, scalar2=None, op0=op0)
            return e.tensor_scalar(out=out, in0=in0, scalar1=s1, scalar2=s2, op0=op0, op1=op1)
        return P.op(eng, fn, reads, writes, pwrites)

    def TT(eng, out, in0, in1, op, reads, writes=(), pwrites=()):
        def fn(e):
            return e.tensor_tensor(out=out, in0=in0, in1=in1, op=op)
        return P.op(eng, fn, reads, writes, pwrites)

    def STT(out, in0, scalar, in1, op0, op1, reads, writes=(), pwrites=()):
        def fn(e):
            return e.scalar_tensor_tensor(out=out, in0=in0, scalar=scalar, in1=in1, op0=op0, op1=op1)
        return P.op("dve", fn, reads, writes, pwrites)

    def CP(eng, out, in_, reads, writes=(), pwrites=()):
        def fn(e):
            return e.tensor_copy(out=out, in_=in_)
        return P.op(eng, fn, reads, writes, pwrites)

    def MSET(eng, ap, val, writes=(), pwrites=()):
        def fn(e):
            return e.memset(ap, val)
        return P.op(eng, fn, (), writes, pwrites)

    def MM(out, lhsT, rhs, start, stop, reads, writes=(), pwrites=()):
        def fn(e):
            return e.matmul(out, lhsT, rhs, start=start, stop=stop, skip_group_check=True)
        return P.op("pe", fn, reads, writes, pwrites)

    def TR(out, in_, ident, reads, writes=(), pwrites=()):
        def fn(e):
            return e.transpose(out, in_, ident)
        return P.op("pe", fn, reads, writes, pwrites)

    def DMA(q, out, in_, reads, writes, dbuf, pwrites=()):
        def fn(e):
            return e.dma_start(out=out, in_=in_)
        return P.op(q, fn, reads, writes, pwrites, dma_buf=dbuf)

    def IOTA(out, pattern, base, cm, writes):
        def fn(e):
            return e.iota(out, pattern, base=base, channel_multiplier=cm,
                          allow_small_or_imprecise_dtypes=True)
        return P.op("pool", fn, (), writes)

    def wload(src3, k, n, wc):
        i = w_rr[0]
        w_rr[0] = (i + 1) % NW
        dst = wsl[i][:, :].rearrange("p (k n) -> p k n", k=k)
        DMA("sp", dst, src3, [wc], [WS[i]], WS[i])
        return i

    WC = [P.buf("wcast%d" % i) for i in range(4)]
    wada3 = w_ada.rearrange("(k p) n -> p k n", p=128)
    ada_slots = []

    def ada_load(n):
        i = w_rr[0]
        w_rr[0] = (i + 1) % NW
        dst = wsl[i][:, :].rearrange("p (k n) -> p k n", k=8)
        DMA("pool", dst, wada3[:, :, n * 512:(n + 1) * 512], [], [WS[i]], WS[i])
        return i

    _tb = {}

    def B(t):
        k = id(t)
        if k not in _tb:
            _tb[k] = P.buf("t%d" % len(_tb))
        return _tb[k]

    _tb[id(kf[0])] = KF[0]
    _tb[id(stS)] = TMPR
    _tb[id(Vblk)] = VB

    def Bs(*ts):
        return [B(t) for t in ts]

    for n in range(NW):
        ada_slots.append(ada_load(n))
    CASTS = ((w_in, wb_in, 768), (w_o, wb_o, 256), (w_mlp1, wb_1, 1024), (w_mlp2, wb_2, 1024))
    for wi in (0, 1, 2, 3):
        src, dst, rows = CASTS[wi]
        for r in range(0, rows, 128):
            DMA("pool", dst[r:r + 128, :], src[r:r + 128, :], [], [], WC[wi], [WC[wi]])
    MSET("dve", stage[:, :], 0.0, Bs(stage))
    DMA("sp", stage[0:8, :], norm1_g, [], Bs(stage), B(stage))
    DMA("sp", stage[8:16, :], norm2_g, [], [], B(stage), Bs(stage))
    DMA("sp", stage[16:28, :], conv_w, [], [], B(stage), Bs(stage))
    DMA("sp", stage[28:29, :], subln_g, [], [], B(stage), Bs(stage))
    for kc in range(8):
        DMA("sp", stage[32 + kc * 5:32 + kc * 5 + 5, :], cc[:, kc * 128:(kc + 1) * 128], [], [], B(stage),
            Bs(stage))
    for i in range(4):
        DMA("sp", lamt[:, i * 64:(i + 1) * 64], lam_in[i].partition_broadcast(128), [], [], B(lamt), Bs(lamt))
    DMA("sp", FG[:, :], final_g.partition_broadcast(128), [], Bs(FG), B(FG))
    DMA("sp", stS[:, :], sconv, [], Bs(stS), B(stS))


    IOTA(iot[:, :], [[1, 128]], 0, -1, Bs(iot))
    TS("dve", ident_f[:, :], iot[:, :], 0.0, None, ALU.is_equal, None, Bs(iot), Bs(ident_f))
    CP("dve", identb[:, :], ident_f[:, :], Bs(ident_f), Bs(identb))
    MSET("dve", ones_f[:, :], 1.0, Bs(ones_f))
    MSET("dve", onesm_b[:, :], 1.0 / 128.0, Bs(onesm_b))
    MSET("dve", mhalf[:, :], -0.5, Bs(mhalf))
    MSET("dve", epsT[:, :], EPS, Bs(epsT))
    MSET("dve", cst[:, :], 0.0, CST)
    tabi = T("tabi", [128, NTAB], F32)
    IOTA(tabi[:, :], [[-128, NTAB]], 128 * TABOFF, 1, Bs(tabi))
    for h in range(4):
        TS("dve", tab[:, h * NTAB:(h + 1) * NTAB], tabi[:, :], SLOPES[h], None, ALU.mult, None,
           Bs(tabi), [], Bs(tab))
    tabsi = T("tabsi", [128, 4], F32)
    IOTA(tabsi[:, :], [[-32, 4]], 0, 1, Bs(tabsi))
    for h in range(4):
        TS("dve", tabS[:, h * 4:(h + 1) * 4], tabsi[:, :], SLOPES[h], None, ALU.mult, None,
           Bs(tabsi), [], Bs(tabS))
    for h in range(4):
        TS("dve", e0f[:, :], iot[:, :], 0.0, 2.0 * SLOPES[h], ALU.min, ALU.mult, Bs(iot), Bs(e0f))
        MSET("dve", e0f[64:128, 0:64], -BIG, Bs(e0f))
        CP("dve", E0b[:, h * 128:(h + 1) * 128], e0f[:, :], Bs(e0f), [], Bs(E0b))
    esi = T("esi", [128, 32], F32)
    mska = T("mska", [128, 32], F32)
    for s in range(4):
        IOTA(esi[:, :], [[1, 32]], 32 * s, -1, Bs(esi))
        IOTA(mska[:, :], [[0, 32]], -32 * s, 1, Bs(mska))
        TS("dve", msk2[:, :], mska[:, :], 0.0, None, ALU.is_ge, None, Bs(mska), Bs(msk2))
        TS("dve", msk[:, :], mska[:, :], 32.0, None, ALU.is_lt, None, Bs(mska), Bs(msk))
        TT("dve", msk[:, :], msk[:, :], msk2[:, :], ALU.mult, Bs(msk, msk2), Bs(msk))
        TS("dve", msk[:, :], msk[:, :], -1.0, BIG, ALU.add, ALU.mult, Bs(msk), Bs(msk))
        for h in range(4):
            TS("dve", msk2[:, :], esi[:, :], 0.0, 2.0 * SLOPES[h], ALU.min, ALU.mult, Bs(esi), Bs(msk2))
            TT("dve", Esb[:, (h * 4 + s) * 32:(h * 4 + s + 1) * 32], msk2[:, :], msk[:, :], ALU.add,
               Bs(msk, msk2), [], Bs(Esb))
    seli = T("seli", [8, 128], F32)
    IOTA(seli[:, :], [[0, 128]], 0, 1, Bs(seli))
    TS("dve", selp[:, :], seli[:, :], 0.0, None, ALU.is_equal, None, Bs(seli), Bs(selp))
    IOTA(selt[:, :], [[1, 128]], 32, -32, Bs(selt))
    TS("dve", selt2[:, :], selt[:, :], 0.0, None, ALU.is_ge, None, Bs(selt), Bs(selt2))
    TS("dve", sels[:, :], selt[:, :], 32.0, None, ALU.is_lt, None, Bs(selt), Bs(sels))
    TT("dve", sels[:, :], sels[:, :], selt2[:, :], ALU.mult, Bs(sels, selt2), Bs(sels))

    for c in range(2):
        MSET("dve", KTblk[c][64:66, :], 1.0, [], [KTB])
        for i in range(NSL):
            MSET("dve", sK[i][c][64:66, :], 1.0, [], [SL[i]])
    IOTA(qrow[0:1, :].rearrange("p (a r) -> p a r", a=4), [[-128, 4], [0, 128]], 0, 0, Bs(qrow))
    P.op("pool", (lambda e: e.iota(qrow[32:33, :].rearrange("p (a r) -> p a r", a=4), [[0, 4], [-1, 128]],
                                   base=0, channel_multiplier=0, allow_small_or_imprecise_dtypes=True)),
         [], [], Bs(qrow))
    for which in range(2):
        for h in range(4):
            TS("dve", qrowb[32 * which:32 * which + 1, h * 512:(h + 1) * 512], qrow[32 * which:32 * which + 1, :],
               SLOPES[h], None, ALU.mult, None, Bs(qrow), [], Bs(qrowb))
    for c in range(2):
        for which in range(2):
            DMA("sp", QTc[c][64 + which:65 + which, :], qrowb[32 * which:32 * which + 1, :], Bs(qrowb), [],
                B(qrowb), QTB)

    b0 = next_bank()
    TR(PS[b0][:, 0:128], stage[:, :], ident_f[:, :], Bs(stage, ident_f), [PSB[b0]])
    CP("dve", prmT[:, :], PS[b0][:, 0:128], [PSB[b0]], Bs(prmT))
    ACT(siluT[:, :], prmT[:, 32:72], AF.Silu, Bs(prmT), Bs(siluT))
    TS("dve", gsub[:, :], prmT[:, 28:29], 1.0 - LAM_INIT, None, ALU.mult, None, Bs(prmT), Bs(gsub))
    TS("dve", FG[:, :], FG[:, :], 32.0, None, ALU.mult, None, Bs(FG), Bs(FG))
    for i in range(2):
        TT("dve", lamp[:, :], lamt[:, (2 * i) * 64:(2 * i + 1) * 64], lamt[:, (2 * i + 1) * 64:(2 * i + 2) * 64],
           ALU.mult, Bs(lamt), Bs(lamp))
        P.op("dve", (lambda e, i=i: e.reduce_sum(out=lams[:, i:i + 1], in_=lamp[:, :], axis=AX.X)),
             Bs(lamp), [], Bs(lams))
    lame = T("lame", [128, 2], F32)
    ACT(lame[:, :], lams[:, 0:2], AF.Exp, Bs(lams), Bs(lame))
    TT("dve", neg_lam[:, :], lame[:, 1:2], lame[:, 0:1], ALU.subtract, Bs(lame), Bs(neg_lam))
    TS("dve", neg_lam[:, :], neg_lam[:, :], -LAM_INIT, None, ALU.add, None, Bs(neg_lam), Bs(neg_lam))

    for n in range(12):
        si = ada_slots[n] if n < NW else ada_load(n)
        b = next_bank()
        for kc in range(8):
            MM(PS[b][0:5, :], siluT[:, kc * 5:(kc + 1) * 5], wsl[si][:, kc * 512:(kc + 1) * 512],
               kc == 0, kc == 7, Bs(siluT) + [WS[si]], [PSB[b]] if kc == 0 else [], [] if kc == 0 else [PSB[b]])
        DMA("sp", badc[n % 2][0:5, :], b_ada[n * 512:(n + 1) * 512].partition_broadcast(5), [], [BADC[n % 2]],
            BADC[n % 2])
        TT("dve", modS[0:5, n * 512:(n + 1) * 512], PS[b][0:5, :], badc[n % 2][0:5, :], ALU.add,
           [PSB[b], BADC[n % 2]], [], MODB)
    b = next_bank()
    chunks = list(range(0, 8)) + list(range(8, 16)) + list(range(24, 32)) + list(range(32, 40))
    for idx, ch in enumerate(chunks):
        TR(PS[b][:, idx * 5:(idx + 1) * 5], modS[0:5, ch * 128:(ch + 1) * 128], ident_f[0:5, 0:5],
           MODB + Bs(ident_f), [PSB[b]] if idx == 0 else [], [] if idx == 0 else [PSB[b]])
    CP("dve", modT[:, :], PS[b][:, 0:160], [PSB[b]], Bs(modT))
    for which in range(2):
        sc0 = 40 if which == 0 else 120
        for kc in range(8):
            TS("dve", GS[:, which * 40 + kc * 5: which * 40 + kc * 5 + 5], modT[:, sc0 + kc * 5: sc0 + kc * 5 + 5],
               1.0, prmT[:, which * 8 + kc: which * 8 + kc + 1], ALU.add, ALU.mult, Bs(modT, prmT), [], Bs(GS))
    TS("dve", GS[:, :], GS[:, :], 32.0, None, ALU.mult, None, Bs(GS), Bs(GS))

    def SHv(which, kc, s):
        c = (0 if which == 0 else 80) + kc * 5 + s
        return modT[:, c:c + 1]

    def GSv(which, kc, s):
        c = which * 40 + kc * 5 + s
        return GS[:, c:c + 1]

    for grp in range(2):
        sel = selp if grp == 0 else sels
        for gi, (Gt, c0) in enumerate(((G1[grp], 16 * 128), (G2[grp], 40 * 128))):
            for hf in range(2):
                b = next_bank()
                MM(PS[b][:, :], sel[0:5, :], modS[0:5, c0 + hf * 512: c0 + (hf + 1) * 512], True, True,
                   MODB + Bs(sel), [PSB[b]])
                CP("dve", Gt[:, hf * 512:(hf + 1) * 512], PS[b][:, :], [PSB[b]], [], Bs(Gt) + ([GBs] if grp else []))
    cstS = T("cstS", [128, 32], F32)
    b = next_bank()
    for j in range(4):
        TR(PS[b][:, j * 8:(j + 1) * 8], stS[:, j * 128:(j + 1) * 128], ident_f[0:8, 0:8],
           Bs(stS, ident_f), [PSB[b]] if j == 0 else [], [] if j == 0 else [PSB[b]])
    CP("dve", cstS[:, :], PS[b][:, 0:32], [PSB[b]], Bs(cstS))

    bar = T("bar", [128, 1], F32)
    P.op("dve", (lambda e: e.memset(bar[:, :], 1.0)), list(_tb.values()) + MODB, [MISC, PRM, LAMB, CONSTB, GB])

    xbt = [[xb[:, t * D:(t + 1) * D] for t in range(4)],
           [G1[1][:, :], G2[1][:, :], yout[:, :], xb2d[:, :]]]
    junkv = [rl[i][:, :].bitcast(BF16) for i in range(2)]

    def stat(kind, t, j):
        c = (kind * 4 + t) * 3 + j
        return stats[:, c:c + 1], STB[c]

    def norm_stats(xset, t, kind):
        q, Q = stat(kind, t, 0)
        a_, A = stat(kind, t, 1)
        r, R = stat(kind, t, 2)
        i = t % 2
        ACT(junkv[i], xbt[xset][t], AF.Square, XBs[xset][t], [Q, RL[i]], accum=q)
        TS("pool", a_, q, D * EPS, None, ALU.add, None, [Q], [A])
        TT("pool", r, a_, mhalf[:, 0:1], ALU.pow, [A, MISC], [R])

    def norm_copy(xset, t, kind):
        r, R = stat(kind, t, 2)
        ACT(xn[t][:, :], xbt[xset][t], AF.Copy, XBs[xset][t] + [R], [XN[t]], scale=r)

    def norm_tr(NT, which, segs):
        TTn = NT // 128
        banks = {}

        def tr(t):
            bks = [next_bank(), next_bank()]
            banks[t] = bks
            pbs = [PS[bk][:, :].bitcast(BF16) for bk in bks]
            for kc in range(8):
                par, kk = kc % 2, kc // 2
                TR(pbs[par][:, kk * 128:(kk + 1) * 128], xn[t][:, kc * 128:(kc + 1) * 128], identb[:, :],
                   [XN[t], MISC], [PSB[bks[par]]] if kk == 0 else [], [] if kk == 0 else [PSB[bks[par]]])

        def ev(t):
            bks = banks[t]
            pbs = [PS[bk][:, :].bitcast(BF16) for bk in bks]
            for kc in range(8):
                par, kk = kc % 2, kc // 2
                for (s_, c0, ncol) in segs:
                    lo = max(c0, t * 128)
                    hi = min(c0 + ncol, (t + 1) * 128)
                    if lo >= hi:
                        continue
                    src = pbs[par][:, kk * 128 + lo - t * 128: kk * 128 + hi - t * 128]
                    dst = hT[kc][:, lo:hi]
                    if par == 0:
                        ACT(dst, src, AF.Identity, [PSB[bks[0]], PRM], [], [HT[kc]],
                            bias=SHv(which, kc, s_), scale=GSv(which, kc, s_))
                    else:
                        TS("dve", dst, src, GSv(which, kc, s_), SHv(which, kc, s_), ALU.mult, ALU.add,
                           [PSB[bks[1]], PRM], [], [HT[kc]])

        tr(0)
        for t in range(TTn):
            if t + 1 < TTn:
                tr(t + 1)
            ev(t)

    unit_rr = [0]

    def attn_unit(h, c0, nq, ktiles, pending=None):
        qt = [QTc[c][0:66, h * 512 + c0: h * 512 + c0 + nq] for c in range(2)]
        sb = [[0, 1], [2, 3]]
        up = unit_rr[0] % 2
        unit_rr[0] += 1
        ob = [4 + 2 * up, 5 + 2 * up]
        if up == 0:
            zacc = [uS[0][:, 0:nq], uS[1][:, 0:nq]]
            ZB = [US[0], US[1]]
        else:
            zacc = [rl[0][:, 0:nq], rl[1][:, 0:nq]]
            ZB = [RL[0], RL[1]]
        n = len(ktiles)
        first = [True, True]

        def emit_S(ti):
            kt = ktiles[ti]
            nk, qlo = kt["nk"], kt["qlo"]
            for c in range(2):
                b = sb[ti % 2][c]
                MM(PS[b][0:nk, qlo:nq], kt["KT"][c], qt[c][:, qlo:nq],
                   True, kt["E"] is None, kt["reads"] + [QTB[h]], [PSB[b]])
                if kt["E"] is not None:
                    Eap, ecol, ew = kt["E"]
                    MM(PS[b][0:nk, ecol:ecol + ew], identb[0:nk, 0:nk], Eap, False, True,
                       [MISC], [], [PSB[b]])

        def emit_rest(ti):
            kt = ktiles[ti]
            nk, qlo = kt["nk"], kt["qlo"]
            assert nk == 128
            for c in range(2):
                b = sb[ti % 2][c]
                pi = (2 * ti + c) % 4
                ACT(pT[pi][0:nk, qlo:nq], PS[b][0:nk, qlo:nq], AF.Exp, [PSB[b], MISC], [PTB[pi]],
                    bias=kt["bias"])
                st = first[c]
                first[c] = False
                MM(PS[ob[c]][:, qlo:nq], kt["V"], pT[pi][0:nk, qlo:nq], st, ti == n - 1,
                   kt["vreads"] + [PTB[pi]], [PSB[ob[c]]] if st else [], [] if st else [PSB[ob[c]]])
                if st:
                    assert qlo == 0
                    CP("dve", zacc[c], pT[pi][:, 0:nq], [PTB[pi]], [ZB[c]])
                else:
                    TT("dve", zacc[c][:, qlo:nq], zacc[c][:, qlo:nq], pT[pi][:, qlo:nq], ALU.add,
                       [PTB[pi], ZB[c]], [ZB[c]])

        emit_S(0)
        for ti in range(n):
            if ti + 1 < n:
                emit_S(ti + 1)
            emit_rest(ti)
            if "after" in ktiles[ti]:
                ktiles[ti]["after"]()
            if pending is not None and ti == min(3, n - 1):
                pending[0](ti % 2)
            if pending is not None and ti == min(6, n - 1):
                pending[1](ti % 2)
                pending = None

        def fin_a(fp):
            _finalize_a(nq, sb[fp], ob, zacc, ZB)

        def fin_b(fp):
            _finalize_b(h, c0, nq, sb[fp])
        return (fin_a, fin_b)

    def _finalize_a(nq, fb, ob, zacc, ZB):
        r0, r1 = cvt[0][:, 0:nq], cvt[1][:, 0:nq]
        a2, a3 = uc[0][:, 0:nq], uc[1][:, 0:nq]
        for c in range(2):
            zbk = fb[c]
            MM(PS[zbk][:, 0:nq], ones_f[:, :], zacc[c], True, True, [ZB[c], MISC], [PSB[zbk]])
            ACT((r0, r1)[c], PS[zbk][:, 0:nq], AF.Ln, [PSB[zbk]], [CVT[c]])
        for c in range(2):
            ACT((r0, r1)[c], (r0, r1)[c], AF.Exp, [CVT[c]], [CVT[c]], scale=-1.0)
        TT("dve", a2, PS[ob[0]][:, 0:nq], r0, ALU.mult, [PSB[ob[0]], CVT[0]], [UC[0]])
        TT("dve", a3, PS[ob[1]][:, 0:nq], r1, ALU.mult, [PSB[ob[1]], CVT[1]], [UC[1]])
        a4 = r0
        STT(a4, a3, neg_lam[:, :], a2, ALU.mult, ALU.add, [UC[0], UC[1], LAMB], [CVT[0]])
        TT("pool", sqb[:, 0:nq], a4, a4, ALU.mult, [CVT[0]], [SQB])

    def _finalize_b(h, c0, nq, fb):
        r1 = cvt[1][:, 0:nq]
        a2 = uc[0][:, 0:nq]
        a4 = cvt[0][:, 0:nq]
        b = fb[0]
        MM(PS[b][:, 0:nq], onesm_b[:, :], sqb[:, 0:nq], True, True, [SQB, MISC], [PSB[b]])
        ACT(r1, PS[b][:, 0:nq], AF.Ln, [PSB[b], MISC], [CVT[1]], bias=epsT[:, :])
        ACT(a2, r1, AF.Exp, [CVT[1]], [UC[0]], scale=-0.5)
        STT(mixT[4 + h][:, c0:c0 + nq], a4, gsub[:, :], a2, ALU.mult, ALU.mult, [CVT[0], UC[0], PRM],
            [], [MIX[4 + h]])

    def mk_ctx(is_sample, bi):
        c = {}
        if is_sample:
            c.update(NT=NSS * DEC_SEQ, NSEG=NSS, L=DEC_SEQ, xsrc=xs, ysrc=y_s, ksrc=k_s, vsrc=v_s, r0=0,
                     segs=[(1 + s_, s_ * DEC_SEQ, DEC_SEQ) for s_ in range(NSS)], grp=1, xset=0)
        else:
            c.update(NT=512, NSEG=1, L=512, xsrc=xp, ysrc=y_p, ksrc=k_p, vsrc=v_p, r0=bi * 512,
                     segs=[(0, 0, 512)], grp=0, xset=(bi + 1) % 2)
        c["is_sample"] = is_sample
        c["bi"] = bi
        return c

    def phase_load(c):
        xset = c["xset"]
        for t in range(c["NT"] // 128):
            extra = [GBs] if (xset == 1 and t < 2) else []
            DMA("sp", xbt[xset][t], c["xsrc"][c["r0"] + t * 128: c["r0"] + (t + 1) * 128, :], [],
                XBs[xset][t] + extra, XBs[xset][t][0])

    def phase_norm1a(c):
        for t in range(c["NT"] // 128):
            norm_stats(c["xset"], t, 0)
        for t in range(c["NT"] // 128):
            norm_copy(c["xset"], t, 0)

    def phase_norm1b(c):
        norm_tr(c["NT"], 0, c["segs"])

    def phase_main(cx, pre=None):
        is_sample, bi, NT, NSEG, L, grp, xset = (cx["is_sample"], cx["bi"], cx["NT"], cx["NSEG"], cx["L"], cx["grp"],
                                                 cx["xset"])
        ksrc, vsrc, r0 = cx["ksrc"], cx["vsrc"], cx["r0"]
        GBr = GBs if is_sample else GB
        TTn = NT // 128
        LP = L + 2
        win3 = wb_in_m.rearrange("(k p) n -> p k n", p=128)
        su = wload(win3[:, :, 1024:1536], 8, 512, WC[0])
        sc_ = wload(win3[:, :, 512:1024], 8, 512, WC[0])
        sB = wload(win3[:, :, 0:512], 8, 512, WC[0])
        for j in range(4):
            i2 = j % 2
            bu, bc, bb = next_bank(), next_bank(), next_bank()
            for (slot, b) in ((su, bu), (sc_, bc), (sB, bb)):
                for kc in range(8):
                    MM(PS[b][:, 0:NT], wsl[slot][:, kc * 512 + j * 128: kc * 512 + (j + 1) * 128], hT[kc][:, 0:NT],
                       kc == 0, kc == 7, [WS[slot], HT[kc]], [PSB[b]] if kc == 0 else [],
                       [] if kc == 0 else [PSB[b]])
            ACT(uS[i2][:, 0:NT], PS[bu][:, 0:NT], AF.Copy, [PSB[bu]], [US[i2]])
            uc3 = uc[i2][:, 0:NSEG * LP].rearrange("p (s l) -> p s l", s=NSEG)
            if is_sample:
                st3 = cstS[:, j * 8:(j + 1) * 8].rearrange("p (s r) -> p s r", s=NSEG)
                CP("pool", uc3[:, :, 0:2], st3, [PRM], [UC[i2]])
            else:
                st3 = cst[:, j * 8: j * 8 + 2].rearrange("p (s r) -> p s r", s=1)
                CP("pool", uc3[:, :, 0:2], st3, [CST[j]], [UC[i2]])
            TT("dve", uc3[:, :, 2:LP], PS[bc][:, 0:NT].rearrange("p (s l) -> p s l", s=NSEG),
               uS[i2][:, 0:NT].rearrange("p (s l) -> p s l", s=NSEG), ALU.mult,
               [PSB[bc], US[i2]], [], [UC[i2]])
            if is_sample:
                CP("pool", cfin[:, j * 8:(j + 1) * 8].rearrange("p (s r) -> p s r", s=NSEG), uc3[:, :, L:LP],
                   [UC[i2]], [], [CFIN])
            else:
                CP("pool", cst[:, j * 8: j * 8 + 2].rearrange("p (s r) -> p s r", s=1), uc3[:, :, L:LP],
                   [UC[i2]], [CST[j]])
            cv3 = cvt[i2][:, 0:NT].rearrange("p (s l) -> p s l", s=NSEG)

            def cw(tap, j=j):
                return prmT[:, 16 + tap * 4 + j: 16 + tap * 4 + j + 1]
            TS("dve", cv3, uc3[:, :, 2:LP], cw(2), None, ALU.mult, None, [UC[i2], PRM], [CVT[i2]])
            for tap in (1, 0):
                def fn(e, tap=tap, cv3=cv3, uc3=uc3, cw=cw):
                    return e.scalar_tensor_tensor(out=cv3, in0=uc3[:, :, tap:tap + L], scalar=cw(tap), in1=cv3,
                                                  op0=ALU.mult, op1=ALU.add)
                P.op("dve", fn, [UC[i2], PRM, CVT[i2]], [CVT[i2]])
            TT("dve", mixT[j][:, 0:NT], PS[bb][:, 0:NT], cvt[i2][:, 0:NT], ALU.mult, [PSB[bb], CVT[i2]], [MIX[j]])
        _chk(12)
        sq_ = wload(win3[:, :, 1536:2048], 8, 512, WC[0])
        for h in range(4):
            for c in range(2):
                b = next_bank()
                for kc in range(8):
                    w0 = kc * 512 + h * 128 + c * 64
                    MM(PS[b][0:64, 0:NT], wsl[sq_][:, w0:w0 + 64], hT[kc][:, 0:NT],
                       kc == 0, kc == 7, [WS[sq_], HT[kc]], [PSB[b]] if kc == 0 else [], [] if kc == 0 else [PSB[b]])
                ACT(QTc[c][0:64, h * 512: h * 512 + NT], PS[b][0:64, 0:NT], AF.Copy, [PSB[b]], [], [QTB[h]],
                    scale=0.125)
        _chk(13)
        sk_ = wload(win3[:, :, 2048:2560], 8, 512, WC[0])
        sv_ = wload(win3[:, :, 2560:3072], 8, 512, WC[0])
        for t in range(TTn):
            i2 = t % 2
            b = next_bank()
            for kc in range(8):
                MM(PS[b][:, :], hT[kc][:, t * 128:(t + 1) * 128], wsl[sk_][:, kc * 512:(kc + 1) * 512],
                   kc == 0, kc == 7, [WS[sk_], HT[kc]], [PSB[b]] if kc == 0 else [], [] if kc == 0 else [PSB[b]])
            ACT(kf[i2][:, :], PS[b][:, :], AF.Copy, [PSB[b]], [KF[i2]])
            CP("dve", kb[i2][:, :], kf[i2][:, :], [KF[i2]], [KB[i2]])
            DMA("pool", ksrc[r0 + t * 128: r0 + (t + 1) * 128, :], kf[i2][:, :], [KF[i2]], [], KF[i2])
            bv = next_bank()
            for kc in range(8):
                MM(PS[bv][:, :], hT[kc][:, t * 128:(t + 1) * 128], wsl[sv_][:, kc * 512:(kc + 1) * 512],
                   kc == 0, kc == 7, [WS[sv_], HT[kc]], [PSB[bv]] if kc == 0 else [], [] if kc == 0 else [PSB[bv]])
            for c in range(2):
                b2 = next_bank()
                pb = PS[b2][:, :].bitcast(BF16)
                for h in range(4):
                    hc = h * 2 + c
                    TR(pb[0:64, h * 128:(h + 1) * 128], kb[i2][:, hc * 64:(hc + 1) * 64], identb[:, :],
                       [KB[i2], MISC], [PSB[b2]] if h == 0 else [], [] if h == 0 else [PSB[b2]])
                dst = KTblk[c][0:64, :].rearrange("p (h k) -> p h k", h=4)[:, :, t * 128:(t + 1) * 128]
                srcv = pb[0:64, 0:512].rearrange("p (h k) -> p h k", h=4)
                if c:
                    CP("dve", dst, srcv, [PSB[b2]], [], [KTB])
                else:
                    ACT(dst, srcv, AF.Copy, [PSB[b2]], [], [KTB])
            b = bv
            ACT(vf[i2][:, :], PS[b][:, :], AF.Copy, [PSB[b]], [VF[i2]])
            vdst = Vblk[:, :].rearrange("p (h t d) -> p h t d", h=4, t=4)[:, :, t, :]
            CP("dve", vdst, vf[i2][:, :].rearrange("p (h d) -> p h d", h=4), [VF[i2]], [], [VB])
            DMA("pool", vsrc[r0 + t * 128: r0 + (t + 1) * 128, :], vf[i2][:, :], [VF[i2]], [], VF[i2])
        _chk(14)
        if (not is_sample) and bi < NB - 1:
            for c in range(2):
                DMA("pool", kt_scr[bi * 128 + c * 64: bi * 128 + (c + 1) * 64, :], KTblk[c][0:64, :], [KTB],
                    [KSCR[bi]] if c == 0 else [], KTB, [] if c == 0 else [KSCR[bi]])
            DMA("pool", v_scr[bi * 128:(bi + 1) * 128, :], Vblk[:, :], [VB], [VSCR[bi]], VB)
        if is_sample or bi == NB - 1:
            ns2 = NSEG * 2
            b = next_bank()
            for j in range(4):
                src = cfin[:, j * 8: j * 8 + ns2] if is_sample else cst[:, j * 8: j * 8 + 2]
                TR(PS[b][0:ns2, j * 128:(j + 1) * 128], src, ident_f[:, :],
                   ([CFIN] if is_sample else [CST[j]]) + [MISC], [PSB[b]] if j == 0 else [],
                   [] if j == 0 else [PSB[b]])
            CP("dve", cfo[0:ns2, :], PS[b][0:ns2, :], [PSB[b]], [CFO])
            DMA("pool", (conv_s if is_sample else conv_p), cfo[0:ns2, :], [CFO], [], CFO)

        _chk(1 if is_sample else 101)
        pend = [None]
        if is_sample:
            for s in range(NSS):
                if pend[0] is not None:
                    pend[0][0](0)
                    pend[0][1](0)
                    pend[0] = None
                DMA("pool", Kraw.rearrange("p (t f) -> p t f", t=PT),
                    ck[s * PAST:(s + 1) * PAST, :].rearrange("(t p) f -> p t f", p=128), [], KRB, KRB[0])
                for h in range(4):
                    DMA("pool", Vsm[:, h * PT * 128:(h + 1) * PT * 128].rearrange("p (t d) -> p t d", t=PT),
                        cv[s * PAST:(s + 1) * PAST, h * 128:(h + 1) * 128].rearrange("(t p) d -> p t d", p=128),
                        [], VSB if h == 0 else [], VSB[0], [] if h == 0 else VSB)
                for c in range(2):
                    MSET("dve", KTs[c][64:66, :], 1.0, [], KSB)
                for h in range(4):
                    for c in range(2):
                        for t0 in range(0, PT, 8):
                            tn = min(8, PT - t0)
                            b = next_bank()
                            pb = PS[b][:, :].bitcast(BF16)
                            for tt in range(tn):
                                t = t0 + tt
                                k0 = t * 512 + h * 128 + c * 64
                                TR(pb[0:64, tt * 128:(tt + 1) * 128], Kraw[:, k0:k0 + 64],
                                   identb[:, :], KRB + [MISC], [PSB[b]] if tt == 0 else [],
                                   [] if tt == 0 else [PSB[b]])
                            CP("dve", KTs[c][0:64, h * PT * 128 + t0 * 128: h * PT * 128 + (t0 + tn) * 128],
                               pb[0:64, 0:tn * 128], [PSB[b]], [], KSB)
                for h in range(4):
                    kts = []
                    for t in range(PT):
                        kts.append(dict(
                            KT=[KTs[c][:, h * PT * 128 + t * 128: h * PT * 128 + (t + 1) * 128] for c in range(2)],
                            V=Vsm[:, (h * PT + t) * 128:(h * PT + t + 1) * 128], nk=128, qlo=0,
                            bias=tab[:, h * NTAB + (PT - t) + TABOFF: h * NTAB + (PT - t) + TABOFF + 1],
                            E=None, reads=KSB, vreads=VSB))
                    kts.append(dict(
                        KT=[KTblk[c][:, h * 512: h * 512 + 128] for c in range(2)],
                        V=Vblk[:, h * 512: h * 512 + 128], nk=128, qlo=0,
                        bias=tabS[:, h * 4 + s: h * 4 + s + 1],
                        E=(Esb[:, (h * 4 + s) * 32:(h * 4 + s + 1) * 32], 0, 32), reads=[KTB], vreads=[VB]))
                    pend[0] = attn_unit(h, s * DEC_SEQ, DEC_SEQ, kts, pend[0])
        else:
            for h in range(4):
                kts = []
                base = sl_rr[0]
                sl_rr[0] = (base + bi) % NSL

                def fill(bprev, h=h, base=base):
                    si = (base + bprev) % NSL
                    for c in range(2):
                        DMA("sp", sK[si][c][0:64, :],
                            kt_scr[bprev * 128 + c * 64: bprev * 128 + (c + 1) * 64, h * 512:(h + 1) * 512],
                            [KSCR[bprev]], [SL[si]] if c == 0 else [], SL[si], [] if c == 0 else [SL[si]])
                    DMA("sp", sV[si][:, :], v_scr[bprev * 128:(bprev + 1) * 128, h * 512:(h + 1) * 512],
                        [VSCR[bprev]], [], SL[si], [SL[si]])

                for bprev in range(bi):
                    si = (base + bprev) % NSL
                    for t in range(4):
                        jj = 4 * (bi - bprev) - t
                        kt = dict(
                            KT=[sK[si][c][:, t * 128:(t + 1) * 128] for c in range(2)],
                            V=sV[si][:, t * 128:(t + 1) * 128], nk=128, qlo=0,
                            bias=tab[:, h * NTAB + jj + TABOFF: h * NTAB + jj + TABOFF + 1],
                            E=None, reads=[SL[si]], vreads=[SL[si]])
                        if t == 3 and bprev + NSL < bi:
                            kt["after"] = (lambda b=bprev + NSL, fill=fill: fill(b))
                        kts.append(kt)
                for bprev in range(min(NSL, bi)):
                    fill(bprev)
                for t in range(4):
                    kts.append(dict(
                        KT=[KTblk[c][:, h * 512 + t * 128: h * 512 + (t + 1) * 128] for c in range(2)],
                        V=Vblk[:, (h * 4 + t) * 128:(h * 4 + t + 1) * 128], nk=128, qlo=t * 128,
                        bias=tab[:, h * NTAB - t + TABOFF: h * NTAB - t + TABOFF + 1],
                        E=(E0b[:, h * 128:(h + 1) * 128], t * 128, 128), reads=[KTB], vreads=[VB]))
                pend[0] = attn_unit(h, 0, 512, kts, pend[0])
                if h == 0 and pre is not None:
                    phase_load(pre)

        if pend[0] is not None:
            pend[0][0](0)
            pend[0][1](0)
        _chk(2 if is_sample else 102)
        wo3 = wb_o_m.rearrange("(k p) n -> p k n", p=128)
        sos = [wload(wo3[:, :, hf * 512:(hf + 1) * 512], 8, 512, WC[1]) for hf in range(2)]
        for t in range(TTn):
            for hf in range(2):
                so = sos[hf]
                b = next_bank()
                for kc in range(8):
                    MM(PS[b][:, :], mixT[kc][:, t * 128:(t + 1) * 128], wsl[so][:, kc * 512:(kc + 1) * 512],
                       kc == 0, kc == 7, [WS[so], MIX[kc]], [PSB[b]] if kc == 0 else [], [] if kc == 0 else [PSB[b]])
                TT("dve", tmpr[:, :], PS[b][:, :], G1[grp][:, hf * 512:(hf + 1) * 512], ALU.mult,
                   [PSB[b], GBr], [TMPR])
                xs_ = xbt[xset][t][:, hf * 512:(hf + 1) * 512]
                TT("dve", xs_, xs_, tmpr[:, :], ALU.add, [TMPR, XBs[xset][t][hf]], [XBs[xset][t][hf]])
            norm_stats(xset, t, 1)
            if t >= 1:
                norm_copy(xset, t - 1, 1)
        norm_copy(xset, TTn - 1, 1)
        norm_tr(NT, 1, cx["segs"])
        if pre is not None:
            phase_norm1a(pre)
        _chk(3 if is_sample else 103)
        w13 = wb_1_m.rearrange("(k p) n -> p k n", p=128)
        for g in range(8):
            s1 = wload(w13[:, :, g * 512:(g + 1) * 512], 8, 512, WC[2])
            for j in range(4):
                jj = g * 4 + j
                i2 = jj % 2
                b = next_bank()
                for kc in range(8):
                    MM(PS[b][:, 0:NT], wsl[s1][:, kc * 512 + j * 128: kc * 512 + (j + 1) * 128], hT[kc][:, 0:NT],
                       kc == 0, kc == 7, [WS[s1], HT[kc]], [PSB[b]] if kc == 0 else [], [] if kc == 0 else [PSB[b]])
                ACT(rl[i2][:, 0:NT], PS[b][:, 0:NT], AF.Relu, [PSB[b]], [RL[i2]])
                TT("pool", HID[:, jj * 512: jj * 512 + NT], rl[i2][:, 0:NT], rl[i2][:, 0:NT], ALU.mult,
                   [RL[i2]], [HB[jj]])
    def phase_tail(c, pre=None):
        is_sample, NT, grp, xset, ysrc, r0 = c["is_sample"], c["NT"], c["grp"], c["xset"], c["ysrc"], c["r0"]
        GBr = GBs if is_sample else GB
        TTn = NT // 128
        if pre is not None:
            phase_norm1b(pre)
        w23 = wb_2_m.rearrange("(j p) n -> p j n", p=128)
        for g in range(8):
            s2 = wload(w23[:, g * 4:(g + 1) * 4, :], 4, 1024, WC[3])
            for j in range(4):
                jj = g * 4 + j
                for t in range(TTn):
                    for hf in range(2):
                        b = t * 2 + hf
                        MM(PS[b][:, :], HID[:, jj * 512 + t * 128: jj * 512 + (t + 1) * 128],
                           wsl[s2][:, j * 1024 + hf * 512: j * 1024 + (hf + 1) * 512],
                           jj == 0, jj == 31, [WS[s2], HB[jj]], [PSB[b]] if jj == 0 else [],
                           [] if jj == 0 else [PSB[b]])
        ps_rr[0] = 0

        def fin_out(t):
            xt = xbt[xset][t]
            r, R = stat(2, t, 2)
            STT(xt, xt, r, FG[:, :], ALU.mult, ALU.mult, XBs[xset][t] + [R, LAMB], XBs[xset][t])
            DMA("pool", ysrc[r0 + t * 128: r0 + (t + 1) * 128, :], xt, XBs[xset][t], [], XBs[xset][t][1])

        for t in range(TTn):
            for hf in range(2):
                b = t * 2 + hf
                TT("dve", tmpr[:, :], PS[b][:, :], G2[grp][:, hf * 512:(hf + 1) * 512], ALU.mult,
                   [PSB[b], GBr], [TMPR])
                xs_ = xbt[xset][t][:, hf * 512:(hf + 1) * 512]
                TT("dve", xs_, xs_, tmpr[:, :], ALU.add, [TMPR, XBs[xset][t][hf]], [XBs[xset][t][hf]])
            norm_stats(xset, t, 2)
            if t >= 1:
                fin_out(t - 1)
        fin_out(TTn - 1)

    KSCR = [P.buf("kscr%d" % i) for i in range(NB)]
    VSCR = [P.buf("vscr%d" % i) for i in range(NB)]
    sl_rr = [0]

    try:
        _chk(0)
        ctxs = [mk_ctx(True, 0)] + [mk_ctx(False, bi) for bi in range(NB)]
        phase_load(ctxs[0])
        phase_norm1a(ctxs[0])
        phase_norm1b(ctxs[0])
        for i, c in enumerate(ctxs):
            nxt = ctxs[i + 1] if i + 1 < len(ctxs) else None
            if nxt is not None and i >= 1:
                phase_main(c, nxt)
                phase_tail(c, nxt)
            elif nxt is not None:
                phase_main(c)
                phase_tail(c)
                phase_load(nxt)
                phase_norm1a(nxt)
                phase_norm1b(nxt)
            else:
                phase_main(c)
                phase_tail(c)
            _chk(4 + i)
    except _Stop:
        pass

    P.emit()
    stack.close()
    return nc


_CACHE = {}


def kernel(x_prompt, x_sample, cache_k, cache_v, state_conv, c_prompt, c_sample,
           norm1_g, norm2_g, w_ada, b_ada, w_in, conv_w, lambda_q1, lambda_k1, lambda_q2, lambda_k2,
           subln_g, w_o, w_mlp1, w_mlp2, final_g):
    f = lambda a: np.ascontiguousarray(np.asarray(a), dtype=np.float32)
    x_prompt, x_sample, cache_k, cache_v, state_conv = map(f, (x_prompt, x_sample, cache_k, cache_v, state_conv))
    c_prompt, c_sample = f(c_prompt), f(c_sample)
    NCORE, SEQ, _ = x_prompt.shape
    DECB, TS_, _ = x_sample.shape
    PAST = cache_k.shape[2]
    assert DECB == NSS * NCORE and TS_ == DEC_SEQ
    key = (SEQ, PAST)
    if key not in _CACHE:
        _CACHE[key] = build(SEQ, PAST)
    nc = _CACHE[key]
    shared = {
        "norm1_g": f(norm1_g).reshape(8, 128), "norm2_g": f(norm2_g).reshape(8, 128),
        "w_ada": f(w_ada).reshape(D, 6 * D), "b_ada": f(b_ada).reshape(6 * D),
        "w_in": f(w_in).reshape(768, 4096), "conv_w": f(conv_w).reshape(12, 128),
        "lq1": f(lambda_q1).reshape(64), "lk1": f(lambda_k1).reshape(64),
        "lq2": f(lambda_q2).reshape(64), "lk2": f(lambda_k2).reshape(64),
        "subln_g": f(subln_g).reshape(1, 128), "w_o": f(w_o).reshape(256, 4096),
        "w_mlp1": f(w_mlp1).reshape(1024, 4096), "w_mlp2": f(w_mlp2).reshape(1024, 4096),
        "final_g": f(final_g).reshape(D),
    }
    in_maps = []
    for c in range(NCORE):
        sl = slice(NSS * c, NSS * (c + 1))
        m = dict(shared)
        m["xp"] = x_prompt[c]
        m["xs"] = x_sample[sl].reshape(NSS * DEC_SEQ, D)
        m["ck"] = cache_k[0, sl].reshape(NSS * PAST, 512)
        m["cv"] = cache_v[0, sl].reshape(NSS * PAST, 512)
        m["sconv"] = state_conv[0, sl].reshape(NSS * 2, 512)
        m["cc"] = np.ascontiguousarray(np.concatenate([c_prompt[c:c + 1], c_sample[sl]], axis=0))
        in_maps.append(m)
    res = run_bass_kernel_spmd(nc, in_maps, core_ids=list(range(NCORE)))
    R = res.results
    g = lambda name: np.stack([np.asarray(R[c][name], dtype=np.float32) for c in range(NCORE)], axis=0)
    y_prompt = g("y_p")
    y_sample = g("y_s").reshape(DECB, DEC_SEQ, D)
    k_prompt = g("k_p").reshape(1, NCORE, SEQ, NH, 128)
    v_prompt = g("v_p").reshape(1, NCORE, SEQ, NH, 128)
    conv_prompt = g("conv_p").reshape(1, NCORE, 2, CONV_CH)
    k_sample = g("k_s").reshape(1, DECB, DEC_SEQ, NH, 128)
    v_sample = g("v_s").reshape(1, DECB, DEC_SEQ, NH, 128)
    conv_sample = g("conv_s").reshape(1, DECB, 2, CONV_CH)
    return (y_prompt, y_sample, k_prompt, v_prompt, conv_prompt, k_sample, v_sample, conv_sample)
```
